# Optimizing a Trainium2 kernel written in Bass

```python
import math
import jax
import jax.numpy as jnp
from jax import lax
import numpy as np

D_MODEL = 1024
BATCH = 8
SEQ = 4096
DEPTH = 4

GRID_W = 64
CTX_LEN = 256
N_MIXERS = 4
MIX_WIDTH = D_MODEL
GROUP_W = MIX_WIDTH // N_MIXERS
D_FF = 4 * D_MODEL
CONV_K = 5
CHUNK = 64
Q_BLOCK = 128
EPS = 1e-6
ROPE_THETA = 10000.0

GDN_HEADS = 4
GDN_HEAD_DIM = GROUP_W // GDN_HEADS
GDN_IN = 4 * GROUP_W + 4 * GDN_HEADS
S5_GROUP = 16
S5_GROUPS = GROUP_W // S5_GROUP
S5_STATE = 64
S5_IN = GROUP_W
SSD_HEAD_DIM = 64
SSD_HEADS = GROUP_W // SSD_HEAD_DIM
SSD_GROUPS = 2
SSD_STATE = 128
SSD_CONV_CH = GROUP_W + 2 * SSD_GROUPS * SSD_STATE
SSD_IN = GROUP_W + SSD_CONV_CH + 2 * SSD_HEADS
MLA_HEADS = 4
MLA_NOPE = 64
MLA_ROPE = 32
MLA_QK = MLA_NOPE + MLA_ROPE
MLA_V = GROUP_W // MLA_HEADS
MLA_Q_RANK = 256
MLA_KV_RANK = 128
MLA_IN = MLA_Q_RANK + MLA_KV_RANK + MLA_ROPE

IN_WIDTHS = (GDN_IN, S5_IN, SSD_IN, MLA_IN)
IN_WIDTH = GDN_IN + S5_IN + SSD_IN + MLA_IN
F32 = jnp.float32

kernel_name = 'hybrid_parallel_heads_dit'


def _split(t, widths):
    offs = np.cumsum(widths)[:-1].tolist()
    return jnp.split(t, offs, axis=-1)


def _rms(x, g):
    xf = x.astype(F32)
    y = xf * lax.rsqrt(jnp.mean(jnp.square(xf), axis=-1, keepdims=True) + EPS)
    return (y * g.astype(F32)).astype(x.dtype)


def _l2norm(x):
    return x * lax.rsqrt(jnp.sum(jnp.square(x), axis=-1, keepdims=True) + EPS)


def _modulate(h, g, shift, scale):
    return _rms(h, g) * (1.0 + scale) + shift


def _dwconv(x, w):
    k, ch = w.shape
    return lax.conv_general_dilated(x, w[:, None, :].astype(x.dtype), window_strides=(1,),
                                    padding=[(k // 2, k // 2)],
                                    dimension_numbers=('NWC', 'WIO', 'NWC'),
                                    feature_group_count=ch)


def _bidirectional(scan_fn, ctx_inputs, lat_inputs, state0):
    y_ctx, y_lat = [], []
    for direction in range(2):
        rev = (lambda t: jnp.flip(t, axis=1)) if direction else (lambda t: t)
        oc, s_ctx = scan_fn(direction, tuple(rev(t) for t in ctx_inputs), state0)
        ol, _ = scan_fn(direction, tuple(rev(t) for t in lat_inputs), s_ctx)
        y_ctx.append(rev(oc))
        y_lat.append(rev(ol))
    return y_ctx[0] + y_ctx[1], y_lat[0] + y_lat[1]


def _to_chunks(t):
    b, tlen, h = t.shape[:3]
    t = t.reshape((b, tlen // CHUNK, CHUNK, h) + t.shape[3:])
    return jnp.moveaxis(t, 3, 1)


def gated_delta_rule(q, k, v, g, beta, s0):
    bsz, tlen, nh, dk = q.shape
    qc = _to_chunks(q * dk ** -0.5)
    kc = _to_chunks(k)
    vc = _to_chunks(v)
    gc = jnp.cumsum(_to_chunks(g), axis=-1)
    bc = _to_chunks(beta)[..., None]
    tri = jnp.tril(jnp.ones((CHUNK, CHUNK), dtype=bool))
    strict = jnp.tril(jnp.ones((CHUNK, CHUNK), dtype=bool), -1)
    decay = jnp.exp(jnp.where(tri, gc[..., :, None] - gc[..., None, :], -jnp.inf))
    kb = kc * bc
    a = jnp.where(strict, jnp.einsum('bhncd,bhnsd->bhncs', kb, kc) * decay, 0.0)
    eye = jnp.eye(CHUNK, dtype=a.dtype)
    t_inv = lax.linalg.triangular_solve(a + eye, jnp.broadcast_to(eye, a.shape), left_side=True,
                                        lower=True, unit_diagonal=True)
    u = t_inv @ (vc * bc)
    w = t_inv @ (kb * jnp.exp(gc)[..., None])
    qk = jnp.where(tri, jnp.einsum('bhncd,bhnsd->bhncs', qc, kc) * decay, 0.0)

    def step(s, xs):
        q_i, k_i, u_i, w_i, g_i, qk_i = xs
        v_new = u_i - w_i @ s
        o = (q_i * jnp.exp(g_i)[..., None]) @ s + qk_i @ v_new
        g_last = g_i[..., -1:]
        s = s * jnp.exp(g_last)[..., None] + jnp.einsum(
            'bhcd,bhce->bhde', k_i * jnp.exp(g_last - g_i)[..., None], v_new)
        return s, o

    xs = tuple(jnp.moveaxis(t, 2, 0) for t in (qc, kc, u, w, gc, qk))
    s_fin, o = lax.scan(step, s0, xs)
    o = jnp.moveaxis(o, 0, 2).reshape(bsz, nh, tlen, -1)
    return jnp.swapaxes(o, 1, 2), s_fin


def gdn_mixer(p_ctx, p_lat, conv_w, a_log, dt_bias, norm_g):
    dtype = p_lat.dtype

    def prep(p):
        bsz, tlen, _ = p.shape
        qkv, gate, a, b = _split(p, (3 * GROUP_W, GROUP_W, 2 * GDN_HEADS, 2 * GDN_HEADS))
        qkv = jax.nn.silu(_dwconv(qkv, conv_w)).astype(F32)
        q, k, v = [t.reshape(bsz, tlen, GDN_HEADS, GDN_HEAD_DIM) for t in jnp.split(qkv, 3, axis=-1)]
        a = a.astype(F32).reshape(bsz, tlen, 2, GDN_HEADS)
        g = -jnp.exp(a_log.astype(F32)) * jax.nn.softplus(a + dt_bias.astype(F32))
        beta = jax.nn.sigmoid(b.astype(F32).reshape(bsz, tlen, 2, GDN_HEADS))
        return (_l2norm(q), _l2norm(k), v, g, beta), gate

    def scan_fn(direction, inputs, s0):
        q, k, v, g, beta = inputs
        return gated_delta_rule(q, k, v, g[:, :, direction], beta[:, :, direction], s0)

    in_ctx, gate_ctx = prep(p_ctx)
    in_lat, gate_lat = prep(p_lat)
    s0 = jnp.zeros((p_ctx.shape[0], GDN_HEADS, GDN_HEAD_DIM, GDN_HEAD_DIM), F32)
    o_ctx, o_lat = _bidirectional(scan_fn, in_ctx, in_lat, s0)

    def out(o, gate):
        bsz, tlen = gate.shape[:2]
        gate = jax.nn.silu(gate.astype(F32)).reshape(bsz, tlen, GDN_HEADS, GDN_HEAD_DIM)
        return (_rms(o, norm_g) * gate).reshape(bsz, tlen, GROUP_W).astype(dtype)

    return out(o_ctx, gate_ctx), out(o_lat, gate_lat)


def _linear_combine(e1, e2):
    a1, b1 = e1
    a2, b2 = e2
    return a1 * a2, a2 * b1 + b2


def s5_mixer(u_ctx, u_lat, a_re, a_im, log_step, b_re, b_im, c_re, c_im, d_skip, w_glu, b_glu):
    dtype = u_lat.dtype
    lam = lax.complex(a_re.astype(F32), a_im.astype(F32))
    lam_bar = jnp.exp(lam * jnp.exp(log_step.astype(F32))[..., None])
    b_bar = ((lam_bar - 1.0) / lam)[..., None] * lax.complex(b_re.astype(F32), b_im.astype(F32))
    c_mat = lax.complex(c_re.astype(F32), c_im.astype(F32))

    def scan_fn(direction, inputs, h0):
        (u,) = inputs
        bsz, tlen, _ = u.shape
        ug = u.astype(F32).reshape(bsz, tlen, S5_GROUPS, S5_GROUP).astype(jnp.complex64)
        lb = lam_bar[direction]
        bu = jnp.einsum('gnc,btgc->btgn', b_bar[direction], ug)
        bu = bu.at[:, 0].add(lb * h0)
        _, h = lax.associative_scan(_linear_combine, (jnp.broadcast_to(lb, bu.shape), bu), axis=1)
        y = jnp.einsum('gcn,btgn->btgc', c_mat[direction], h).real
        return y.reshape(bsz, tlen, GROUP_W), h[:, -1]

    h0 = jnp.zeros((u_ctx.shape[0], S5_GROUPS, S5_STATE), jnp.complex64)
    y_ctx, y_lat = _bidirectional(scan_fn, (u_ctx,), (u_lat,), h0)

    def out(y, u):
        y = jax.nn.gelu(y + d_skip.astype(F32) * u.astype(F32))
        return (y * jax.nn.sigmoid(y @ w_glu.astype(F32) + b_glu.astype(F32))).astype(dtype)

    return out(y_ctx, u_ctx), out(y_lat, u_lat)


def ssd_scan(x, dt, a, b, c, h0):
    bsz, tlen, nh, hp = x.shape
    ng, ms = b.shape[2], b.shape[3]
    r = nh // ng
    n = tlen // CHUNK
    xc = (x * dt[..., None]).reshape(bsz, n, CHUNK, ng, r, hp)
    bc = b.reshape(bsz, n, CHUNK, ng, ms)
    cc = c.reshape(bsz, n, CHUNK, ng, ms)
    la = (dt * a).reshape(bsz, n, CHUNK, ng, r)
    cum = jnp.cumsum(jnp.transpose(la, (0, 3, 4, 1, 2)), axis=-1)
    tri = jnp.tril(jnp.ones((CHUNK, CHUNK), dtype=bool))
    lmat = jnp.exp(jnp.where(tri, cum[..., :, None] - cum[..., None, :], -jnp.inf))
    cb = jnp.einsum('bnlgm,bnsgm->bgnls', cc, bc)
    y_diag = jnp.einsum('bgnls,bgrnls,bnsgrp->bnlgrp', cb, lmat, xc)
    chunk_states = jnp.einsum('bnsgm,bgrns,bnsgrp->nbgrpm', bc, jnp.exp(cum[..., -1:] - cum), xc)
    chunk_decay = jnp.moveaxis(jnp.exp(cum[..., -1]), -1, 0)

    def step(h, xs):
        st, dec = xs
        return h * dec[..., None, None] + st, h

    h_fin, h_prev = lax.scan(step, h0.reshape(bsz, ng, r, hp, ms), (chunk_states, chunk_decay))
    y_off = jnp.einsum('bnlgm,nbgrpm,bgrnl->bnlgrp', cc, h_prev, jnp.exp(cum))
    y = (y_diag + y_off).reshape(bsz, tlen, nh, hp)
    return y, h_fin.reshape(bsz, nh, hp, ms)


def mamba2_mixer(p_ctx, p_lat, conv_w, conv_b, a_log, dt_bias, d_skip, norm_g):
    dtype = p_lat.dtype

    def prep(p):
        bsz, tlen, _ = p.shape
        z, xbc, dt = _split(p, (GROUP_W, SSD_CONV_CH, 2 * SSD_HEADS))
        xbc = jax.nn.silu(_dwconv(xbc, conv_w) + conv_b).astype(F32)
        xs, bs, cs = _split(xbc, (GROUP_W, SSD_GROUPS * SSD_STATE, SSD_GROUPS * SSD_STATE))
        xs = xs.reshape(bsz, tlen, SSD_HEADS, SSD_HEAD_DIM)
        bs = bs.reshape(bsz, tlen, SSD_GROUPS, SSD_STATE)
        cs = cs.reshape(bsz, tlen, SSD_GROUPS, SSD_STATE)
        dt = jax.nn.softplus(dt.astype(F32).reshape(bsz, tlen, 2, SSD_HEADS) + dt_bias.astype(F32))
        return (xs, bs, cs, dt), z

    a = -jnp.exp(a_log.astype(F32))

    def scan_fn(direction, inputs, h0):
        xs, bs, cs, dt = inputs
        return ssd_scan(xs, dt[:, :, direction], a[direction], bs, cs, h0)

    in_ctx, z_ctx = prep(p_ctx)
    in_lat, z_lat = prep(p_lat)
    h0 = jnp.zeros((p_ctx.shape[0], SSD_HEADS, SSD_HEAD_DIM, SSD_STATE), F32)
    y_ctx, y_lat = _bidirectional(scan_fn, in_ctx, in_lat, h0)

    def out(y, xs, z):
        bsz, tlen = z.shape[:2]
        y = (y + d_skip.astype(F32)[:, None] * xs).reshape(bsz, tlen, GROUP_W)
        return _rms(y * jax.nn.silu(z.astype(F32)), norm_g).astype(dtype)

    return out(y_ctx, in_ctx[0], z_ctx), out(y_lat, in_lat[0], z_lat)


def _axial_rope_angles(rows):
    pos = jnp.arange(rows * GRID_W)
    row = (pos // GRID_W).astype(F32)
    col = (pos % GRID_W).astype(F32)
    n_freq = MLA_ROPE // 4
    inv_freq = ROPE_THETA ** (-jnp.arange(n_freq, dtype=F32) / n_freq)
    ang = jnp.concatenate([row[:, None] * inv_freq, col[:, None] * inv_freq], axis=-1)
    return jnp.cos(ang), jnp.sin(ang)


def _rope(x, cos, sin):
    xp = x.reshape(x.shape[:-1] + (-1, 2))
    x0, x1 = xp[..., 0], xp[..., 1]
    cos, sin = cos[:, None, :], sin[:, None, :]
    return jnp.stack([x0 * cos - x1 * sin, x0 * sin + x1 * cos], axis=-1).reshape(x.shape)


def _attend(q, k, v):
    s = jnp.einsum('bqhd,bkhd->bhqk', q, k) * (MLA_QK ** -0.5)
    p = jax.nn.softmax(s.astype(F32), axis=-1)
    return jnp.einsum('bhqk,bkhd->bqhd', p, v.astype(F32))


def mla_mixer(p_ctx, p_lat, q_norm_g, kv_norm_g, w_uq, w_ukv, q_gain, k_gain, cos, sin):
    dtype = p_lat.dtype

    def qk_norm(t, gain):
        return jnp.concatenate([_rms(t[..., :MLA_NOPE], gain[:MLA_NOPE]),
                                _rms(t[..., MLA_NOPE:], gain[MLA_NOPE:])], axis=-1).astype(F32)

    def prep(p, rotary):
        bsz, tlen, _ = p.shape
        cq, ckv, k_rope = _split(p, (MLA_Q_RANK, MLA_KV_RANK, MLA_ROPE))
        q = (_rms(cq, q_norm_g) @ w_uq).reshape(bsz, tlen, MLA_HEADS, MLA_QK)
        kv = (_rms(ckv, kv_norm_g) @ w_ukv).reshape(bsz, tlen, MLA_HEADS, MLA_NOPE + MLA_V)
        k_rope = jnp.broadcast_to(k_rope[:, :, None, :], (bsz, tlen, MLA_HEADS, MLA_ROPE))
        k = jnp.concatenate([kv[..., :MLA_NOPE], k_rope], axis=-1)
        q, k = qk_norm(q, q_gain), qk_norm(k, k_gain)
        if rotary:
            q = jnp.concatenate([q[..., :MLA_NOPE], _rope(q[..., MLA_NOPE:], cos, sin)], axis=-1)
            k = jnp.concatenate([k[..., :MLA_NOPE], _rope(k[..., MLA_NOPE:], cos, sin)], axis=-1)
        return q, k, kv[..., MLA_NOPE:].astype(F32)

    qc, kc, vc = prep(p_ctx, False)
    ql, kl, vl = prep(p_lat, True)
    o_ctx = _attend(qc, kc, vc)
    k_all = jnp.concatenate([kc, kl], axis=1)
    v_all = jnp.concatenate([vc, vl], axis=1)
    bsz, tlen = ql.shape[:2]
    q_blocks = jnp.moveaxis(ql.reshape(bsz, tlen // Q_BLOCK, Q_BLOCK, MLA_HEADS, MLA_QK), 1, 0)
    o_lat = lax.map(lambda qb: _attend(qb, k_all, v_all), q_blocks)
    o_lat = jnp.moveaxis(o_lat, 0, 1).reshape(bsz, tlen, GROUP_W)
    return o_ctx.reshape(bsz, -1, GROUP_W).astype(dtype), o_lat.astype(dtype)


def _sq_relu_mlp(h, w1, w2):
    return jnp.square(jax.nn.relu(h @ w1)) @ w2


def setup_inputs(seed: int = 0):
    key = jax.random.key(seed)
    ks = iter(jax.random.split(key, 48))

    def nrm(shape, scale):
        return scale * jax.random.normal(next(ks), shape, F32)

    def gain(shape):
        return 1.0 + nrm(shape, 0.02)

    def unif(shape, lo, hi):
        return jax.random.uniform(next(ks), shape, F32, lo, hi)

    def dt_bias(shape):
        dt = jnp.exp(unif(shape, math.log(1e-3), math.log(1e-1)))
        return dt + jnp.log(-jnp.expm1(-dt))

    L = DEPTH
    return {
        'x': nrm((BATCH, SEQ, D_MODEL), 1.0),
        'c': nrm((BATCH, D_MODEL), 1.0),
        'ctx': nrm((BATCH, CTX_LEN, D_MODEL), 1.0),
        'c_ctx': nrm((D_MODEL,), 1.0),
        'w_mod': nrm((L, D_MODEL, 6 * D_MODEL), 0.5 * D_MODEL ** -0.5),
        'b_mod': nrm((L, 6 * D_MODEL), 0.02),
        'norm1_g': gain((L, D_MODEL)),
        'norm2_g': gain((L, D_MODEL)),
        'w_in': nrm((L, D_MODEL, IN_WIDTH), D_MODEL ** -0.5),
        'w_out': nrm((L, MIX_WIDTH, D_MODEL), MIX_WIDTH ** -0.5),
        'w_ff1': nrm((L, D_MODEL, D_FF), D_MODEL ** -0.5),
        'w_ff2': nrm((L, D_FF, D_MODEL), D_FF ** -0.5),
        'gdn_conv_w': nrm((L, CONV_K, 3 * GROUP_W), CONV_K ** -0.5),
        'gdn_a_log': jnp.log(unif((L, 2, GDN_HEADS), 1.0, 16.0)),
        'gdn_dt_bias': dt_bias((L, 2, GDN_HEADS)),
        'gdn_norm_g': gain((L, GDN_HEAD_DIM)),
        's5_a_re': -0.5 + nrm((L, 2, S5_GROUPS, S5_STATE), 0.01),
        's5_a_im': math.pi * jnp.arange(S5_STATE, dtype=F32) + nrm((L, 2, S5_GROUPS, S5_STATE), 0.01),
        's5_log_step': unif((L, 2, S5_GROUPS), math.log(1e-3), math.log(1e-1)),
        's5_b_re': nrm((L, 2, S5_GROUPS, S5_STATE, S5_GROUP), (2 * S5_GROUP) ** -0.5),
        's5_b_im': nrm((L, 2, S5_GROUPS, S5_STATE, S5_GROUP), (2 * S5_GROUP) ** -0.5),
        's5_c_re': nrm((L, 2, S5_GROUPS, S5_GROUP, S5_STATE), (2 * S5_STATE) ** -0.5),
        's5_c_im': nrm((L, 2, S5_GROUPS, S5_GROUP, S5_STATE), (2 * S5_STATE) ** -0.5),
        's5_d': nrm((L, GROUP_W), 0.5),
        's5_w_glu': nrm((L, GROUP_W, GROUP_W), GROUP_W ** -0.5),
        's5_b_glu': nrm((L, GROUP_W), 0.02),
        'ssd_conv_w': nrm((L, CONV_K, SSD_CONV_CH), CONV_K ** -0.5),
        'ssd_conv_b': nrm((L, SSD_CONV_CH), 0.02),
        'ssd_a_log': jnp.log(unif((L, 2, SSD_HEADS), 1.0, 16.0)),
        'ssd_dt_bias': dt_bias((L, 2, SSD_HEADS)),
        'ssd_d': gain((L, SSD_HEADS)),
        'ssd_norm_g': gain((L, GROUP_W)),
        'mla_q_norm_g': gain((L, MLA_Q_RANK)),
        'mla_kv_norm_g': gain((L, MLA_KV_RANK)),
        'mla_w_uq': nrm((L, MLA_Q_RANK, MLA_HEADS * MLA_QK), MLA_Q_RANK ** -0.5),
        'mla_w_ukv': nrm((L, MLA_KV_RANK, MLA_HEADS * (MLA_NOPE + MLA_V)), MLA_KV_RANK ** -0.5),
        'mla_q_gain': gain((L, MLA_QK)),
        'mla_k_gain': gain((L, MLA_QK)),
    }


def reference(x, c, ctx, c_ctx, w_mod, b_mod, norm1_g, norm2_g, w_in, w_out, w_ff1, w_ff2,
              gdn_conv_w, gdn_a_log, gdn_dt_bias, gdn_norm_g,
              s5_a_re, s5_a_im, s5_log_step, s5_b_re, s5_b_im, s5_c_re, s5_c_im, s5_d, s5_w_glu, s5_b_glu,
              ssd_conv_w, ssd_conv_b, ssd_a_log, ssd_dt_bias, ssd_d, ssd_norm_g,
              mla_q_norm_g, mla_kv_norm_g, mla_w_uq, mla_w_ukv, mla_q_gain, mla_k_gain):
    tlen = x.shape[1]
    rows = tlen // GRID_W
    cos, sin = _axial_rope_angles(rows)
    cond_lat = jax.nn.silu(c)[:, None, :]
    cond_ctx = jax.nn.silu(c_ctx)[None, None, :]
    h_lat, h_ctx = x, ctx
    for l in range(DEPTH):
        m_lat = jnp.split(cond_lat @ w_mod[l] + b_mod[l], 6, axis=-1)
        m_ctx = jnp.split(cond_ctx @ w_mod[l] + b_mod[l], 6, axis=-1)
        a_lat = _modulate(h_lat, norm1_g[l], m_lat[0], m_lat[1]) @ w_in[l]
        a_ctx = _modulate(h_ctx, norm1_g[l], m_ctx[0], m_ctx[1]) @ w_in[l]
        gdn_c, s5_c, ssd_c, mla_c = _split(a_ctx, IN_WIDTHS)
        gdn_l, s5_l, ssd_l, mla_l = _split(a_lat, IN_WIDTHS)
        ya = gdn_mixer(gdn_c, gdn_l, gdn_conv_w[l], gdn_a_log[l], gdn_dt_bias[l], gdn_norm_g[l])
        yb = s5_mixer(s5_c, s5_l, s5_a_re[l], s5_a_im[l], s5_log_step[l], s5_b_re[l], s5_b_im[l],
                      s5_c_re[l], s5_c_im[l], s5_d[l], s5_w_glu[l], s5_b_glu[l])
        yc = mamba2_mixer(ssd_c, ssd_l, ssd_conv_w[l], ssd_conv_b[l], ssd_a_log[l], ssd_dt_bias[l],
                          ssd_d[l], ssd_norm_g[l])
        yd = mla_mixer(mla_c, mla_l, mla_q_norm_g[l], mla_kv_norm_g[l], mla_w_uq[l], mla_w_ukv[l],
                       mla_q_gain[l], mla_k_gain[l], cos, sin)
        mix_lat = jnp.concatenate([ya[1], yb[1], yc[1], yd[1]], axis=-1)
        h_lat = h_lat + m_lat[2] * (mix_lat @ w_out[l])
        h_lat = h_lat + m_lat[5] * _sq_relu_mlp(_modulate(h_lat, norm2_g[l], m_lat[3], m_lat[4]),
                                                w_ff1[l], w_ff2[l])
        if l < DEPTH - 1:
            mix_ctx = jnp.concatenate([ya[0], yb[0], yc[0], yd[0]], axis=-1)
            h_ctx = h_ctx + m_ctx[2] * (mix_ctx @ w_out[l])
            h_ctx = h_ctx + m_ctx[5] * _sq_relu_mlp(_modulate(h_ctx, norm2_g[l], m_ctx[3], m_ctx[4]),
                                                    w_ff1[l], w_ff2[l])
    return h_lat
```

```python
from contextlib import ExitStack
import numpy as np
import concourse.bass as bass
import concourse.mybir as mybir
from concourse.bass_utils import run_bass_kernel_spmd

F32 = mybir.dt.float32
BF16 = mybir.dt.bfloat16
AF = mybir.ActivationFunctionType
ALU = mybir.AluOpType
AX = mybir.AxisListType

_WRITE_KW = ("out", "accum_out", "ap")
N_DMA_SEMS = 24

T = 4352
NCTX = 256
NLAT = 4096
D = 1024
DEPTH = 4
EPS = 1e-6
INW = 2744
BLOCKS = [(0, 256)] + [(256 + 512 * i, 512) for i in range(8)]
NT = 34


class Em:
    def __init__(self, nc):
        self.nc = nc
        self.eng = {"pe": nc.tensor, "dve": nc.vector, "act": nc.scalar,
                    "pool": nc.gpsimd, "sp": nc.sync}
        self.names = list(self.eng.keys()) + ["q%d" % i for i in range(N_DMA_SEMS)]
        self.nid = {n: i for i, n in enumerate(self.names)}
        self.sem = {n: nc.alloc_semaphore("s_" + n) for n in self.names}
        self.count = {n: 0 for n in self.names}
        nn = len(self.names)
        self.seen = {n: [0] * nn for n in self.eng}
        self.clock = {n: [None] for n in self.names}
        self.bufs = {}
        self.next_q = 0
        self.n_inst = 0
        self.n_wait = 0

    @staticmethod
    def region(ap):
        t = ap.tensor
        pat = ap.ap
        esz = mybir.dt.size(ap.dtype)
        if "dram" in str(ap.space).lower() or "hbm" in str(ap.space).lower():
            lo = hi = ap.offset
            for st, cnt in pat:
                d = st * (cnt - 1)
                if d < 0:
                    lo += d
                else:
                    hi += d
            return (t.name, 0, 1, lo * esz, (hi + 1) * esz)
        if str(ap.space) == "PSUM":
            return (t.name, 0, 128, -(1 << 60), 1 << 60)
        pstride = pat[0][0]
        p0 = ap.start_partition()
        p1 = p0 + ap.partition_size()
        off = ap.offset - (p0 * pstride if pstride else 0)
        lo = hi = off
        for st, cnt in pat[1:]:
            d = st * (cnt - 1)
            if d < 0:
                lo += d
            else:
                hi += d
        return (t.name, p0, p1, lo * esz, (hi + 1) * esz)

    def _deps(self, reg, is_write, deps):
        lst = self.bufs.get(reg[0])
        if not lst:
            return
        _, p0, p1, lo, hi = reg
        for rec in lst:
            if not (is_write or rec[2]):
                continue
            if rec[3] < p1 and p0 < rec[4] and rec[5] < hi and lo < rec[6]:
                deps.add((rec[0], rec[1]))

    def _record(self, reg, is_write, e, idx):
        name, p0, p1, lo, hi = reg
        lst = self.bufs.setdefault(name, [])
        keep = []
        for rec in lst:
            inside = rec[3] >= p0 and rec[4] <= p1 and rec[5] >= lo and rec[6] <= hi
            if inside and (is_write or (rec[0] == e and not rec[2])):
                continue
            keep.append(rec)
        keep.append((e, idx, is_write, p0, p1, lo, hi))
        if len(keep) > 48:
            best = {}
            for rec in keep:
                if rec[1] > best.get(rec[0], 0):
                    best[rec[0]] = rec[1]
            keep = [(k, v, True, 0, 128, -(1 << 60), 1 << 60) for k, v in best.items()]
        self.bufs[name] = keep

    def _merge(self, e, de, di):
        seen = self.seen[e]
        ck = self.clock[de][di]
        for k in range(len(seen)):
            if ck[k] > seen[k]:
                seen[k] = ck[k]
        j = self.nid[de]
        if seen[j] < di:
            seen[j] = di

    def _sync(self, e, reads, writes):
        deps = set()
        for r in reads:
            self._deps(r, False, deps)
        for w in writes:
            self._deps(w, True, deps)
        seen = self.seen[e]
        eng = self.eng[e]
        for (de, di) in sorted(deps, key=lambda x: -x[1]):
            if de == e and e == "pe":
                continue
            if seen[self.nid[de]] >= di:
                continue
            eng.wait_ge(self.sem[de], di * (16 if de[0] == "q" else 1))
            self.n_wait += 1
            self._merge(e, de, di)

    def op(self, e, fn, *args, **kw):
        reads, writes = [], []
        for k, v in kw.items():
            if isinstance(v, bass.AP):
                (writes if (k in _WRITE_KW or str(v.space) == "PSUM") else reads).append(self.region(v))
        self._sync(e, reads, writes)
        inst = getattr(self.eng[e], fn)(*args, **kw)
        idx = self.count[e] + 1
        self.count[e] = idx
        inst.then_inc(self.sem[e], 1)
        self.clock[e].append(tuple(self.seen[e]))
        for r in reads:
            self._record(r, False, e, idx)
        for w in writes:
            self._record(w, True, e, idx)
        self.n_inst += 1
        return inst

    def dma(self, out, in_, q="sp", **kw):
        r = self.region(in_)
        w = self.region(out)
        qn = "q%d" % self.next_q
        self.next_q = (self.next_q + 1) % N_DMA_SEMS
        self._sync(q, [r], [w])
        m = self.count[qn]
        if m and self.seen[q][self.nid[qn]] < m:
            self.eng[q].wait_ge(self.sem[qn], m * 16)
            self._merge(q, qn, m)
        inst = self.eng[q].dma_start(out=out, in_=in_, **kw)
        inst.then_inc(self.sem[qn], 16)
        idx = m + 1
        self.count[qn] = idx
        self.clock[qn].append(tuple(self.seen[q]))
        self._record(r, False, qn, idx)
        self._record(w, True, qn, idx)
        self.n_inst += 1
        return inst

    def fence(self):
        for e in self.eng:
            for n in self.names:
                c = self.count[n]
                if c == 0 or (n == e and e in ("pe", "sp")):
                    continue
                if self.seen[e][self.nid[n]] >= c:
                    continue
                self.eng[e].wait_ge(self.sem[n], c * (16 if n[0] == "q" else 1))
                self._merge(e, n, c)
        self.bufs = {}

    def mm(self, out, lhsT, rhs, start=True, stop=True):
        return self.op("pe", "matmul", out=out, lhsT=lhsT, rhs=rhs, start=start, stop=stop)

    def tr(self, out, in_, identity):
        return self.op("pe", "transpose", out=out, in_=in_, identity=identity)

    def act(self, out, in_, func, **kw):
        return self.op("act", "activation", out=out, in_=in_, func=func, **kw)

    def ts(self, e, out, in0, s1, s2=None, op0=ALU.mult, op1=None):
        if op1 is None and e == "pool" and op0 in (ALU.mult, ALU.add):
            return self.op(e, "tensor_scalar", out=out, in0=in0, scalar1=s1, scalar2=1.0, op0=op0, op1=ALU.mult)
        if op1 is None:
            return self.op(e, "tensor_scalar", out=out, in0=in0, scalar1=s1, scalar2=None, op0=op0)
        return self.op(e, "tensor_scalar", out=out, in0=in0, scalar1=s1, scalar2=s2, op0=op0, op1=op1)

    def tt(self, e, out, in0, in1, op):
        return self.op(e, "tensor_tensor", out=out, in0=in0, in1=in1, op=op)

    def stt(self, out, in0, scalar, in1, op0, op1):
        return self.op("dve", "scalar_tensor_tensor", out=out, in0=in0, scalar=scalar, in1=in1, op0=op0, op1=op1)

    def copy(self, e, out, in_):
        if e == "act":
            return self.act(out=out, in_=in_, func=AF.Copy)
        return self.op(e, "tensor_copy", out=out, in_=in_)


C_ID, C_TRIU, C_TRIL, C_ONES, C_OFFD, C_BLK, C_SEL, C_EO = 0, 128, 256, 384, 512, 640, 768, 1024
C_BD16, C_M1, C_M2, C_M3 = 1026, 1154, 1282, 1410
C_NEGU, C_NEGL = 1538, 1666
NCONST = 1794


def make_consts():
    c = np.zeros((128, NCONST), np.float32)
    i = np.arange(128)
    c[:, C_ID:C_ID + 128] = np.eye(128)
    c[:, C_TRIU:C_TRIU + 128] = (i[:, None] <= i[None, :])
    c[:, C_TRIL:C_TRIL + 128] = (i[:, None] >= i[None, :])
    c[:, C_ONES:C_ONES + 128] = 1.0
    c[:, C_OFFD:C_OFFD + 128] = 1.0 - np.eye(128)
    c[:, C_BLK:C_BLK + 128] = (i[:, None] // 64 == i[None, :] // 64)
    c[0, C_SEL:C_SEL + 128] = 1.0
    c[1, C_SEL + 128:C_SEL + 256] = 1.0
    c[:, C_EO] = ((i // 16) % 2 == 0)
    c[:, C_EO + 1] = ((i // 16) % 2 == 1)
    bd = lambda b: (i[:, None] // b == i[None, :] // b).astype(np.float32)
    c[:, C_BD16:C_BD16 + 128] = bd(16)
    c[:, C_M1:C_M1 + 128] = bd(32) - bd(16)
    c[:, C_M2:C_M2 + 128] = bd(64) - bd(32)
    c[:, C_M3:C_M3 + 128] = 1.0 - bd(64)
    c[:, C_NEGU:C_NEGU + 128] = -1e5 * (i[:, None] > i[None, :])
    c[:, C_NEGL:C_NEGL + 128] = -1e5 * (i[:, None] < i[None, :])
    return c


class Builder:
    def __init__(self, nlayers=DEPTH, dbg=False):
        self.nlayers = nlayers
        self.dbg = dbg
        nc = self.nc = bass.Bass("TRN2", target_bir_lowering=False)
        self.em = Em(nc)
        L = DEPTH

        def inp(name, shape):
            return nc.dram_tensor(name, list(shape), F32, kind="ExternalInput").ap()

        self.I = {}
        for name, shape in [
            ("hcat", (T, D)), ("cond2", (2, D)), ("consts", (128, NCONST)), ("rope", (NLAT, 32)),
            ("w_mod", (L, D, 6 * D)), ("b_mod", (L, 6 * D)), ("norm1_g", (L, D)), ("norm2_g", (L, D)),
            ("w_in", (L, D, INW)), ("w_out", (L, D, D)), ("w_ff1", (L, D, 4 * D)), ("w_ff2", (L, 4 * D, D)),
            ("gdn_conv_w", (L, 5, 768)), ("gdn_a_log", (L, 8)), ("gdn_dt_bias", (L, 8)), ("gdn_norm_g", (L, 64)),
            ("s5_a_re", (L, 2, 16, 64)), ("s5_a_im", (L, 2, 16, 64)), ("s5_log_step", (L, 2, 16)),
            ("s5_b_re", (L, 2, 16, 64, 16)), ("s5_b_im", (L, 2, 16, 64, 16)),
            ("s5_c_re", (L, 2, 16, 16, 64)), ("s5_c_im", (L, 2, 16, 16, 64)),
            ("s5_d", (L, 256)), ("s5_w_glu", (L, 256, 256)), ("s5_b_glu", (L, 256)),
            ("ssd_conv_w", (L, 5, 768)), ("ssd_conv_b", (L, 768)), ("ssd_a_log", (L, 8)), ("ssd_dt_bias", (L, 8)),
            ("ssd_d", (L, 4)), ("ssd_norm_g", (L, 256)),
            ("mla_q_norm_g", (L, 256)), ("mla_kv_norm_g", (L, 128)), ("mla_w_uq", (L, 256, 384)),
            ("mla_w_ukv", (L, 128, 512)), ("mla_q_gain", (L, 96)), ("mla_k_gain", (L, 96)),
        ]:
            self.I[name] = inp(name, shape)
        self.out = nc.dram_tensor("out", [NLAT, D], F32, kind="ExternalOutput").ap()

        kind = "ExternalOutput" if dbg else "Internal"

        def scr(name, shape, dt=F32):
            return nc.dram_tensor(name, list(shape), dt, kind=kind).ap()

        S = self.S = {}
        for name, shape in [("hbuf", (T, D)), ("tokA", (T, 272)), ("tokZ", (T, 256)), ("tokM", (T, 424))]:
            S[name] = scr(name, shape)
        for name, shape in [("gd_qT", (256, T)), ("gd_kT", (256, T)), ("gd_k", (T, 256)), ("gd_v", (T, 256)),
                            ("s5_uT", (256, T)), ("sd_x", (T, 256)), ("sd_BT", (256, T)), ("sd_B", (T, 256)),
                            ("sd_CT", (256, T))]:
            S[name] = scr(name, shape, BF16)
        S["mixT"] = scr("mixT", (D, T), BF16)
        S["x2T"] = scr("x2T", (D, T), BF16)
        if dbg:
            self.I["mix_dbg"] = nc.dram_tensor("mix_dbg", [D, T], BF16, kind="ExternalInput").ap()

        sb = nc.alloc_sbuf_tensor
        self.cst = sb("cst", [128, NCONST], F32)
        self.mcol = sb("mcol", [128, 4, 8, 2], F32)
        self.G1 = sb("G1", [128, 8, 2], F32)
        self.G2 = sb("G2", [128, 8, 2], F32)
        self.gate = sb("gatebc", [128, 2, 2, D], F32)
        self.condT = sb("condT", [128, 8, 2], F32)
        self.ps = [nc.alloc_psum_tensor("ps%d" % i, [128, 512], F32) for i in range(8)]
        self.ident = self.cst[:, C_ID:C_ID + 128]
        self.identb = sb("identb", [128, 128], BF16)
        self._ev = 0
        self._uid = 0
        self._pb = 0

    @staticmethod
    def _adv(g, n):
        if g is None:
            return None
        for _ in range(n):
            try:
                next(g)
            except StopIteration:
                return None
        return g

    def sbt(self, st, name, shape, dt):
        self._uid += 1
        return st.enter_context(self.nc.sbuf_tensor("%s_%d" % (name, self._uid), shape, dt))

    def evac(self, out, in_):
        self._ev ^= 1
        self.em.copy("act" if self._ev else "dve", out, in_)

    def load_cols(self, dst, src2d, n, tmp, pst):
        em = self.em
        em.dma(out=tmp[0:n, 0:128], in_=src2d)
        em.tr(out=pst[:, 0:n], in_=tmp[0:n, 0:128], identity=self.ident[0:n, 0:n])
        em.copy("dve", dst, pst[:, 0:n])

    def phase0_once(self):
        em, nc = self.em, self.nc
        em.dma(out=self.cst[:], in_=self.I["consts"])
        em.copy("dve", self.identb[:], self.ident)
        with ExitStack() as st:
            c2 = self.sbt(st, "c2", [2, D], F32)
            em.dma(out=c2[:], in_=self.I["cond2"])
            em.act(out=c2[:], in_=c2[:], func=AF.Silu)
            pst = self.ps[0]
            for k in range(8):
                em.tr(out=pst[:, 2 * k:2 * k + 2], in_=c2[0:2, k * 128:(k + 1) * 128], identity=self.ident[0:2, 0:2])
            em.copy("dve", self.condT[:].rearrange("p k r -> p (k r)"), pst[:, 0:16])
            em.fence()

    def phase0(self, l):
        em, nc = self.em, self.nc
        with ExitStack() as st:
            m_sb = self.sbt(st, "m_sb", [2, 6 * D], F32)
            bm = self.sbt(st, "bm", [2, 6 * D], F32)
            wm = [self.sbt(st, "wm%d" % i, [128, 3072], F32) for i in range(2)]
            tmp = self.sbt(st, "p0tmp", [8, 128], F32)
            gcol = self.sbt(st, "gcol", [128, 2, 8], F32)
            for r in range(2):
                em.dma(out=bm[r:r + 1, :], in_=self.I["b_mod"][l:l + 1, :])
            i = 0
            for half in range(2):
                for k in range(8):
                    w = wm[i % 2]
                    i += 1
                    em.dma(out=w[:], in_=self.I["w_mod"][l, k * 128:(k + 1) * 128, half * 3072:(half + 1) * 3072],
                           q=("sp" if k % 2 == 0 else "pool"))
                    for j in range(6):
                        em.mm(out=self.ps[j][0:2, :], lhsT=self.condT[:, k, :], rhs=w[:, j * 512:(j + 1) * 512],
                              start=(k == 0), stop=(k == 7))
                for j in range(6):
                    c0 = half * 3072 + j * 512
                    em.tt("dve", m_sb[0:2, c0:c0 + 512], self.ps[j][0:2, :], bm[0:2, c0:c0 + 512], ALU.add)
            pst = self.ps[6]
            for vi, v in enumerate((0, 1, 3, 4)):
                for k in range(8):
                    c0 = v * D + k * 128
                    o = (vi * 8 + k) * 2
                    em.tr(out=pst[:, o:o + 2], in_=m_sb[0:2, c0:c0 + 128], identity=self.ident[0:2, 0:2])
            em.copy("dve", self.mcol[:].rearrange("p v k r -> p (v k r)"), pst[:, 0:64])
            self.load_cols(gcol[:, 0, :], self.I["norm1_g"][l].rearrange("(k p) -> k p", p=128), 8, tmp, self.ps[7])
            self.load_cols(gcol[:, 1, :], self.I["norm2_g"][l].rearrange("(k p) -> k p", p=128), 8, tmp, self.ps[7])
            for gi, (G, sv) in enumerate(((self.G1, 1), (self.G2, 3))):
                for r in range(2):
                    em.stt(out=G[:, :, r], in0=self.mcol[:, sv, :, r], scalar=1.0, in1=gcol[:, gi, :],
                           op0=ALU.add, op1=ALU.mult)
            n = 0
            for gi, v in enumerate((2, 5)):
                for r in range(2):
                    for hf in range(2):
                        pst = self.ps[n % 6]
                        n += 1
                        c0 = v * D + hf * 512
                        em.mm(out=pst[:, :], lhsT=self.cst[0:2, C_SEL + r * 128:C_SEL + (r + 1) * 128],
                              rhs=m_sb[0:2, c0:c0 + 512])
                        self.evac(self.gate[:, gi, r, hf * 512:(hf + 1) * 512], pst[:, :])
            em.fence()

    def norm_xT(self, st, src, ntiles, xT, G, shcol, tile_row, pfx):
        em, nc = self.em, self.nc
        hin = [self.sbt(st, "%shin%d" % (pfx, i), [128, D], F32) for i in range(2)]
        hs = [self.sbt(st, "%shs%d" % (pfx, i), [128, D], F32) for i in range(2)]
        stat = self.sbt(st, pfx + "stat", [128, 4, ntiles], F32)
        for t in range(ntiles):
            r = tile_row(t)
            ht = hin[t % 2] if not isinstance(src, list) else src[t]
            if not isinstance(src, list):
                em.dma(out=ht[:], in_=src[t * 128:(t + 1) * 128, :], q=("sp" if t % 2 == 0 else "pool"))
            hsb = hs[t % 2]
            em.act(out=hsb[:], in_=ht[:], func=AF.Square, accum_out=stat[:, 0, t:t + 1])
            em.ts("dve", stat[:, 1, t:t + 1], stat[:, 0, t:t + 1], 1.0 / D, EPS, ALU.mult, ALU.add)
            em.act(out=stat[:, 2, t:t + 1], in_=stat[:, 1, t:t + 1], func=AF.Sqrt)
            em.op("dve", "reciprocal", out=stat[:, 3, t:t + 1], in_=stat[:, 2, t:t + 1])
            em.act(out=hsb[:], in_=ht[:], func=AF.Identity, scale=stat[:, 3, t:t + 1])
            for half in range(2):
                pst = self.ps[(2 * t + half) % 4]
                for j in range(4):
                    k = half * 4 + j
                    em.tr(out=pst[:, j * 128:(j + 1) * 128], in_=hsb[:, k * 128:(k + 1) * 128], identity=self.ident)
                for j in range(4):
                    k = half * 4 + j
                    o = xT[:, k, t * 128:(t + 1) * 128]
                    if j % 2 == 0:
                        em.ts("dve", o, pst[:, j * 128:(j + 1) * 128], G[:, k, r:r + 1], shcol[:, k, r:r + 1],
                              ALU.mult, ALU.add)
                    else:
                        em.act(out=o, in_=pst[:, j * 128:(j + 1) * 128], func=AF.Identity,
                               scale=G[:, k, r:r + 1], bias=shcol[:, k, r:r + 1])

    def norm_xT_par(self, st, src, xT, G, shcol, NS=4):
        em = self.em
        slots = [dict(hin=self.sbt(st, "nhin%d" % i, [128, D], F32), hs=self.sbt(st, "nhs%d" % i, [128, D], F32),
                      stat=self.sbt(st, "nstat%d" % i, [128, 4], F32), banks=(self.ps[2 * i], self.ps[2 * i + 1]))
                 for i in range(NS)]

        def tile(t, B):
            r = 0 if t < 2 else 1
            ht, hsb, stat = B["hin"], B["hs"], B["stat"]
            em.dma(out=ht[:], in_=src[t * 128:(t + 1) * 128, :], q=("sp" if t % 2 == 0 else "pool"))
            yield
            em.act(out=hsb[:], in_=ht[:], func=AF.Square, accum_out=stat[:, 0:1])
            yield
            em.ts("dve", stat[:, 1:2], stat[:, 0:1], 1.0 / D, EPS, ALU.mult, ALU.add)
            yield
            em.act(out=stat[:, 2:3], in_=stat[:, 1:2], func=AF.Sqrt)
            yield
            em.op("dve", "reciprocal", out=stat[:, 3:4], in_=stat[:, 2:3])
            yield
            em.act(out=hsb[:], in_=ht[:], func=AF.Identity, scale=stat[:, 3:4])
            yield
            for half in range(2):
                pst = B["banks"][half]
                for j in range(4):
                    k = half * 4 + j
                    em.tr(out=pst[:, j * 128:(j + 1) * 128], in_=hsb[:, k * 128:(k + 1) * 128], identity=self.ident)
            yield
            for half in range(2):
                pst = B["banks"][half]
                for j in range(4):
                    k = half * 4 + j
                    o = xT[:, k, t * 128:(t + 1) * 128]
                    if (j + half) % 2 == 0:
                        em.ts("dve", o, pst[:, j * 128:(j + 1) * 128], G[:, k, r:r + 1], shcol[:, k, r:r + 1],
                              ALU.mult, ALU.add)
                    else:
                        em.act(out=o, in_=pst[:, j * 128:(j + 1) * 128], func=AF.Identity,
                               scale=G[:, k, r:r + 1], bias=shcol[:, k, r:r + 1])
            yield

        pending = list(range(NT))
        active, free = [], list(range(NS))
        while pending or active:
            while pending and free:
                sl_ = free.pop(0)
                active.append((tile(pending.pop(0), slots[sl_]), sl_))
            for item in list(active):
                try:
                    next(item[0])
                except StopIteration:
                    active.remove(item)
                    free.append(item[1])

    def phaseA(self, l):
        em, nc, S = self.em, self.nc, self.S
        hsrc = self.I["hcat"] if l == 0 else S["hbuf"]
        with ExitStack() as st:
            xT = self.sbt(st, "xT", [128, 8, T], BF16)
            with ExitStack() as st1:
                self.norm_xT_par(st1, hsrc, xT, self.G1, self.mcol[:, 0])
            em.fence()
            rowbufs = [self.sbt(st, "rowbuf%d" % i, [128, 4360], F32) for i in range(2)]
            rb = {"cur": rowbufs[0], "n": 0}
            cbuf = self.sbt(st, "cbuf", [128, T], F32)
            tbuf = self.sbt(st, "tbuf", [128, T], F32)
            cbf = self.sbt(st, "cbf", [128, T], BF16)
            wck = [self.sbt(st, "wck%d" % i, [128, 8, 128], BF16) for i in range(2)]
            wtk = self.sbt(st, "wtk", [128, 8, 424], BF16)
            stage = [self.sbt(st, "stg%d" % i, [128, 512], F32) for i in range(2)]
            stageb = [self.sbt(st, "stgb%d" % i, [128, 512], BF16) for i in range(2)]
            cw = self.sbt(st, "cw", [128, 12, 6], F32)
            tmp = self.sbt(st, "atmp", [8, 128], F32)
            w_in = self.I["w_in"][l]

            for ci in range(12):
                src = (self.I["gdn_conv_w"] if ci < 6 else self.I["ssd_conv_w"])[l][:, (ci % 6) * 128:(ci % 6 + 1) * 128]
                em.op("dve", "memset", ap=tmp[:], constant=0.0)
                em.dma(out=tmp[0:5, :], in_=src)
                if ci >= 6:
                    em.dma(out=tmp[5:6, :], in_=self.I["ssd_conv_b"][l:l + 1, (ci - 6) * 128:(ci - 5) * 128])
                em.tr(out=self.ps[7][:, 0:6], in_=tmp[0:6, :], identity=self.ident[0:6, 0:6])
                em.copy("dve", cw[:, ci, :], self.ps[7][:, 0:6])
            for r_ in rowbufs:
                em.op("pool", "memset", ap=r_[:], constant=0.0)

            def col(tok):
                return tok + 2 if tok < NCTX else tok + 6

            state = {"n": 0}

            def proj_feat(c0):
                wc = wck[state["n"] % 2]
                dst = rowbufs[state["n"] % 2]
                state["n"] += 1
                em.dma(out=wc[:], in_=w_in[:, c0:c0 + 128].rearrange("(k p) c -> p k c", p=128), q="pool")
                for bi, (b0, bl) in enumerate(BLOCKS):
                    pst = self.ps[4 + bi % 4]
                    for k in range(8):
                        em.mm(out=pst[:, 0:bl], lhsT=wc[:, k, :], rhs=xT[:, k, b0:b0 + bl], start=(k == 0), stop=(k == 7))
                    em.copy("act", dst[:, col(b0):col(b0) + bl], pst[:, 0:bl])
                return dst

            def conv_silu(ci, to_bf=True):
                for (s0, sl) in ((0, NCTX), (NCTX, NLAT)):
                    c = col(s0)
                    acc = cbuf[:, s0:s0 + sl]
                    em.ts("dve", acc, rb["cur"][:, c - 2:c - 2 + sl], cw[:, ci, 0:1], cw[:, ci, 5:6], ALU.mult, ALU.add)
                    for j in range(1, 5):
                        em.stt(out=acc, in0=rb["cur"][:, c - 2 + j:c - 2 + j + sl], scalar=cw[:, ci, j:j + 1], in1=acc,
                               op0=ALU.mult, op1=ALU.add)
                em.act(out=(cbf[:] if to_bf else cbuf[:]), in_=cbuf[:], func=AF.Silu)

            def to_tokmajor(dst, dc0, act_only=True):
                for g in range(9):
                    t0 = g * 4
                    n = min(4, NT - t0)
                    pst = self.ps[g % 4][:].bitcast(BF16)
                    for j in range(n):
                        em.tr(out=pst[:, j * 128:(j + 1) * 128], in_=cbf[:, (t0 + j) * 128:(t0 + j + 1) * 128],
                              identity=self.identb[:])
                    sg = stageb[g % 2]
                    if act_only:
                        em.copy("act", sg[:, 0:n * 128], pst[:, 0:n * 128])
                    else:
                        self.evac(sg[:, 0:n * 128], pst[:, 0:n * 128])
                    em.dma(out=dst[t0 * 128:(t0 + n) * 128, dc0:dc0 + 128].rearrange("(j p) c -> p j c", p=128),
                           in_=sg[:, 0:n * 128].rearrange("p (j c) -> p j c", c=128))

            def l2norm(scale):
                em.act(out=tbuf[:], in_=cbuf[:], func=AF.Square)
                for bi, (b0, bl) in enumerate(BLOCKS):
                    pst = self.ps[4 + bi % 2]
                    em.mm(out=pst[:, 0:bl], lhsT=self.cst[:, C_BLK:C_BLK + 128], rhs=tbuf[:, b0:b0 + bl])
                    sg = stage[bi % 2]
                    em.act(out=sg[:, 0:bl], in_=pst[:, 0:bl], func=AF.Ln, bias=EPS)
                    em.act(out=sg[:, 0:bl], in_=sg[:, 0:bl], func=AF.Exp, scale=-0.5)
                    em.stt(out=cbf[:, b0:b0 + bl], in0=cbuf[:, b0:b0 + bl], scalar=scale, in1=sg[:, 0:bl],
                           op0=ALU.mult, op1=ALU.mult)

            def post_gdn(ci):
                conv_silu(ci, to_bf=(ci >= 4))
                if ci < 2:
                    l2norm(0.125)
                    em.dma(out=S["gd_qT"][ci * 128:(ci + 1) * 128, :], in_=cbf[:])
                elif ci < 4:
                    l2norm(1.0)
                    em.dma(out=S["gd_kT"][(ci - 2) * 128:(ci - 1) * 128, :], in_=cbf[:])
                    to_tokmajor(S["gd_k"], (ci - 2) * 128, act_only=False)
                else:
                    to_tokmajor(S["gd_v"], (ci - 4) * 128)

            def post_s5(ci):
                for (s0, sl) in ((0, NCTX), (NCTX, NLAT)):
                    em.copy("act", cbf[:, s0:s0 + sl], rb["cur"][:, col(s0):col(s0) + sl])
                em.dma(out=S["s5_uT"][ci * 128:(ci + 1) * 128, :], in_=cbf[:])

            def post_ssd(ci):
                conv_silu(6 + ci)
                if ci < 2:
                    to_tokmajor(S["sd_x"], ci * 128)
                elif ci < 4:
                    em.dma(out=S["sd_BT"][(ci - 2) * 128:(ci - 1) * 128, :], in_=cbf[:])
                    to_tokmajor(S["sd_B"], (ci - 2) * 128)
                else:
                    em.dma(out=S["sd_CT"][(ci - 4) * 128:(ci - 3) * 128, :], in_=cbf[:])

            tasks = ([(ci * 128, post_gdn, ci) for ci in range(6)] + [(1040 + ci * 128, post_s5, ci) for ci in range(2)]
                     + [(1552 + ci * 128, post_ssd, ci) for ci in range(6)])
            bufs_ = [proj_feat(tasks[0][0])]
            for ti, (c0, post, ci) in enumerate(tasks):
                if ti + 1 < len(tasks):
                    bufs_.append(proj_feat(tasks[ti + 1][0]))
                rb["cur"] = bufs_[ti]
                post(ci)
            for (c0, w, dst) in ((768, 272, S["tokA"]), (1296, 256, S["tokZ"]), (2320, 424, S["tokM"])):
                em.dma(out=wtk[:, :, 0:w], in_=w_in[:, c0:c0 + w].rearrange("(k p) c -> p k c", p=128), q="pool")
                for t in range(NT):
                    pst = self.ps[4 + t % 4]
                    for k in range(8):
                        em.mm(out=pst[:, 0:w], lhsT=xT[:, k, t * 128:(t + 1) * 128], rhs=wtk[:, k, 0:w],
                              start=(k == 0), stop=(k == 7))
                    sg = stage[t % 2]
                    self.evac(sg[:, 0:w], pst[:, 0:w])
                    em.dma(out=dst[t * 128:(t + 1) * 128, :], in_=sg[:, 0:w])
            em.fence()

    def rsqrt_cols(self, out, in_, tmp):
        em = self.em
        em.act(out=tmp, in_=in_, func=AF.Sqrt)
        em.op("dve", "reciprocal", out=out, in_=tmp)

    def mixer_mla(self, l, outer=None):
        em, nc, S = self.em, self.nc, self.S
        SC = 96 ** -0.5
        with ExitStack() as st:
            big = outer if outer is not None else st
            QT = self.sbt(big, "QT", [128, 4, T], BF16)
            KT = self.sbt(big, "KT", [128, 4, T], BF16)
            VA = self.sbt(big, "VA", [128, NT, 4, 65], BF16)
            Wuq = self.sbt(st, "Wuq", [128, 2, 384], BF16)
            Wukv = self.sbt(st, "Wukv", [128, 512], BF16)
            gcol = self.sbt(st, "mgcol", [128, 4], F32)
            gbc = self.sbt(st, "mgbc", [128, 2, 96], F32)
            invn = self.sbt(st, "invn", [128, 13], F32)
            tmp = self.sbt(st, "mtmp", [8, 128], F32)
            em.dma(out=Wuq[:], in_=self.I["mla_w_uq"][l].rearrange("(k p) c -> p k c", p=128), q="pool")
            em.dma(out=Wukv[:], in_=self.I["mla_w_ukv"][l], q="pool")
            em.dma(out=tmp[0:2, :], in_=self.I["mla_q_norm_g"][l].rearrange("(k p) -> k p", p=128))
            em.dma(out=tmp[2:3, :], in_=self.I["mla_kv_norm_g"][l:l + 1, :])
            em.dma(out=tmp[3:4, :], in_=self.I["mla_kv_norm_g"][l:l + 1, :])
            em.tr(out=self.ps[7][:, 0:4], in_=tmp[0:4, :], identity=self.ident[0:4, 0:4])
            em.copy("dve", gcol[:], self.ps[7][:, 0:4])
            em.dma(out=gbc[:, 0, :], in_=self.I["mla_q_gain"][l].partition_broadcast(128))
            em.dma(out=gbc[:, 1, :], in_=self.I["mla_k_gain"][l].partition_broadcast(128))
            em.op("dve", "memset", ap=invn[:], constant=1.0 / 64)
            em.op("dve", "memset", ap=invn[:, 4:8], constant=1.0 / 32)
            em.op("dve", "memset", ap=invn[:, 12:13], constant=1.0 / 32)
            em.op("pool", "memset", ap=VA[:], constant=1.0)
            NS = 4
            with ExitStack() as st1:
                slots = []
                for sl_ in range(NS):
                    slots.append(dict(
                        x=self.sbt(st1, "tm%d" % sl_, [128, 424], F32), rr=self.sbt(st1, "rp%d" % sl_, [128, 32], F32),
                        sq=self.sbt(st1, "msq%d" % sl_, [128, 512], F32), cs=self.sbt(st1, "mcs%d" % sl_, [128, 384], F32),
                        cT=self.sbt(st1, "mcT%d" % sl_, [128, 3, 128], BF16), q3=self.sbt(st1, "mq3%d" % sl_, [128, 4, 96], F32),
                        k3=self.sbt(st1, "mk3%d" % sl_, [128, 4, 96], F32), kr=self.sbt(st1, "mkr%d" % sl_, [128, 32], F32),
                        rt=self.sbt(st1, "mrt%d" % sl_, [128, 4, 4, 16], F32), st=self.sbt(st1, "mst%d" % sl_, [128, 4, 16], F32),
                        banks=(self.ps[2 * sl_], self.ps[2 * sl_ + 1])))

                def bc_h(ap2, n):
                    return ap2.unsqueeze(2).broadcast_to([128, 4, n])

                def bc_g(ap2, n):
                    return ap2.unsqueeze(1).broadcast_to([128, 4, n])

                def prep_tile(t, B):
                    lat = t >= 2
                    x, rr, sq, cs, cT, q3, k3, kr, rt, stt_ = (B[k] for k in ("x", "rr", "sq", "cs", "cT", "q3", "k3", "kr", "rt", "st"))
                    bk0, bk1 = B["banks"]
                    em.dma(out=x[:], in_=S["tokM"][t * 128:(t + 1) * 128, :], q=("sp" if t % 2 == 0 else "pool"))
                    if lat:
                        em.dma(out=rr[:], in_=self.I["rope"][(t - 2) * 128:(t - 1) * 128, :])
                    yield
                    em.act(out=sq[:, 0:256], in_=x[:, 8:264], func=AF.Square, accum_out=stt_[:, 0, 0:1])
                    em.act(out=sq[:, 256:384], in_=x[:, 264:392], func=AF.Square, accum_out=stt_[:, 0, 1:2])
                    yield
                    em.ts("dve", stt_[:, 1, 0:1], stt_[:, 0, 0:1], 1.0 / 256, EPS, ALU.mult, ALU.add)
                    em.ts("dve", stt_[:, 1, 1:2], stt_[:, 0, 1:2], 1.0 / 128, EPS, ALU.mult, ALU.add)
                    yield
                    em.act(out=stt_[:, 3, 0:2], in_=stt_[:, 1, 0:2], func=AF.Sqrt)
                    yield
                    em.op("dve", "reciprocal", out=stt_[:, 2, 0:2], in_=stt_[:, 3, 0:2])
                    yield
                    em.act(out=cs[:, 0:256], in_=x[:, 8:264], func=AF.Identity, scale=stt_[:, 2, 0:1])
                    em.act(out=cs[:, 256:384], in_=x[:, 264:392], func=AF.Identity, scale=stt_[:, 2, 1:2])
                    yield
                    for j in range(3):
                        em.tr(out=bk0[:, j * 128:(j + 1) * 128], in_=cs[:, j * 128:(j + 1) * 128], identity=self.ident)
                    yield
                    for j in range(3):
                        em.ts("dve", cT[:, j, :], bk0[:, j * 128:(j + 1) * 128], gcol[:, j:j + 1], None, ALU.mult)
                    yield
                    for k in range(2):
                        em.mm(out=bk1[:, 0:384], lhsT=cT[:, k, :], rhs=Wuq[:, k, :], start=(k == 0), stop=(k == 1))
                    em.mm(out=bk0[:, :], lhsT=cT[:, 2, :], rhs=Wukv[:, :])
                    yield
                    kv3 = bk0[:, :].rearrange("p (h c) -> p h c", c=128)
                    em.copy("act", q3[:].rearrange("p h c -> p (h c)"), bk1[:, 0:384])
                    em.copy("act", k3[:, :, 0:64], kv3[:, :, 0:64])
                    em.copy("dve", VA[:, t, :, 0:64], kv3[:, :, 64:128])
                    yield
                    s3 = sq[:, 0:384].rearrange("p (h c) -> p h c", c=96)
                    em.tt("dve", s3, q3[:], q3[:], ALU.mult)
                    em.act(out=sq[:, 384:416], in_=x[:, 392:424], func=AF.Square, accum_out=stt_[:, 0, 12:13])
                    yield
                    em.op("dve", "tensor_reduce", out=stt_[:, 0, 0:4], in_=s3[:, :, 0:64], axis=AX.X, op=ALU.add)
                    em.op("dve", "tensor_reduce", out=stt_[:, 0, 4:8], in_=s3[:, :, 64:96], axis=AX.X, op=ALU.add)
                    yield
                    em.tt("dve", sq[:, 0:256].rearrange("p (h c) -> p h c", c=64), k3[:, :, 0:64], k3[:, :, 0:64], ALU.mult)
                    yield
                    em.op("dve", "tensor_reduce", out=stt_[:, 0, 8:12],
                          in_=sq[:, 0:256].rearrange("p (h c) -> p h c", c=64), axis=AX.X, op=ALU.add)
                    yield
                    em.tt("dve", stt_[:, 1, 0:13], stt_[:, 0, 0:13], invn[:], ALU.mult)
                    yield
                    em.ts("dve", stt_[:, 1, 0:13], stt_[:, 1, 0:13], EPS, None, ALU.add)
                    yield
                    em.act(out=stt_[:, 3, 0:13], in_=stt_[:, 1, 0:13], func=AF.Sqrt)
                    yield
                    em.op("dve", "reciprocal", out=stt_[:, 2, 0:13], in_=stt_[:, 3, 0:13])
                    yield
                    rs = stt_[:, 2, :]
                    em.tt("dve", q3[:, :, 0:64], q3[:, :, 0:64], bc_h(rs[:, 0:4], 64), ALU.mult)
                    em.tt("dve", q3[:, :, 64:96], q3[:, :, 64:96], bc_h(rs[:, 4:8], 32), ALU.mult)
                    em.tt("dve", k3[:, :, 0:64], k3[:, :, 0:64], bc_h(rs[:, 8:12], 64), ALU.mult)
                    em.stt(out=kr[:], in0=x[:, 392:424], scalar=rs[:, 12:13], in1=gbc[:, 1, 64:96],
                           op0=ALU.mult, op1=ALU.mult)
                    yield
                    em.tt("pool", q3[:], q3[:], bc_g(gbc[:, 0, :], 96), ALU.mult)
                    em.tt("pool", k3[:, :, 0:64], k3[:, :, 0:64], bc_g(gbc[:, 1, 0:64], 64), ALU.mult)
                    yield
                    if lat:
                        cosb, sinb = rr[:, 0:16], rr[:, 16:32]
                        qv = q3[:, :, 64:96].rearrange("p h (i two) -> p h i two", two=2)
                        x0, x1 = qv[:, :, :, 0], qv[:, :, :, 1]
                        em.tt("dve", rt[:, 0], x0, bc_g(cosb, 16), ALU.mult)
                        em.tt("pool", rt[:, 1], x1, bc_g(sinb, 16), ALU.mult)
                        em.tt("dve", rt[:, 2], x0, bc_g(sinb, 16), ALU.mult)
                        em.tt("pool", rt[:, 3], x1, bc_g(cosb, 16), ALU.mult)
                        yield
                        em.tt("dve", x0, rt[:, 0], rt[:, 1], ALU.subtract)
                        em.tt("dve", x1, rt[:, 2], rt[:, 3], ALU.add)
                        yield
                        kvw = kr[:].rearrange("p (i two) -> p i two", two=2)
                        k0, k1 = kvw[:, :, 0], kvw[:, :, 1]
                        em.tt("dve", rt[:, 0, 0], k0, cosb, ALU.mult)
                        em.tt("pool", rt[:, 1, 0], k1, sinb, ALU.mult)
                        em.tt("dve", rt[:, 2, 0], k0, sinb, ALU.mult)
                        em.tt("pool", rt[:, 3, 0], k1, cosb, ALU.mult)
                        yield
                        em.tt("dve", k0, rt[:, 0, 0], rt[:, 1, 0], ALU.subtract)
                        em.tt("dve", k1, rt[:, 2, 0], rt[:, 3, 0], ALU.add)
                        yield
                    em.copy("pool", k3[:, :, 64:96], bc_g(kr[:], 32))
                    yield
                    for h in range(4):
                        em.tr(out=bk1[0:96, h * 128:(h + 1) * 128], in_=q3[:, h, :], identity=self.ident)
                    for h in range(4):
                        em.tr(out=bk0[0:96, h * 128:(h + 1) * 128], in_=k3[:, h, :], identity=self.ident)
                    yield
                    em.copy("act", QT[0:96, :, t * 128:(t + 1) * 128], bk1[0:96, :].rearrange("p (h c) -> p h c", c=128))
                    em.copy("dve", KT[0:96, :, t * 128:(t + 1) * 128], bk0[0:96, :].rearrange("p (h c) -> p h c", c=128))
                    yield

                pending = list(range(NT))
                active = []
                free = list(range(NS))
                while pending or active:
                    while pending and free:
                        sl_ = free.pop(0)
                        active.append((prep_tile(pending.pop(0), slots[sl_]), sl_))
                    for item in list(active):
                        try:
                            next(item[0])
                        except StopIteration:
                            active.remove(item)
                            free.append(item[1])
            em.fence()
            if outer is None:
                for _ in self.mla_attn((QT, KT, VA), st, (0, 1, 2), (4, 5), (6, 7)):
                    pass
            em.fence()
        return QT, KT, VA

    def mla_attn(self, M, st, sb, ob, bb):
        em, S = self.em, self.S
        QT, KT, VA = M
        SC = 96 ** -0.5
        pT = [self.sbt(st, "pT%d" % i, [128, 512], BF16) for i in range(3)]
        osb = [self.sbt(st, "osb%d" % i, [128, 512], F32) for i in range(2)]
        rl = [self.sbt(st, "arl%d" % i, [128, 512], F32) for i in range(2)]
        yd = [self.sbt(st, "yd%d" % i, [64, 512], BF16) for i in range(2)]

        def gen():
            n = 0
            nb = 0
            for h in range(4):
                for (b0, bl) in BLOCKS:
                    kts = list(range(2)) if b0 < NCTX else list(range(NT))
                    po = self.ps[ob[nb % len(ob)]]

                    def s_mm(j):
                        kt = kts[j]
                        em.mm(out=self.ps[sb[(n + j) % len(sb)]][:, 0:bl], lhsT=KT[0:96, h, kt * 128:(kt + 1) * 128],
                              rhs=QT[0:96, h, b0:b0 + bl])
                    s_mm(0)
                    for i, kt in enumerate(kts):
                        if i + 1 < len(kts):
                            s_mm(i + 1)
                        pss = self.ps[sb[(n + i) % len(sb)]]
                        p_ = pT[(n + i) % 3]
                        em.act(out=p_[:, 0:bl], in_=pss[:, 0:bl], func=AF.Exp, scale=SC)
                        em.mm(out=po[0:65, 0:bl], lhsT=VA[:, kt, h, :], rhs=p_[:, 0:bl],
                              start=(i == 0), stop=(i == len(kts) - 1))
                        yield
                    n += len(kts)
                    o_, r_, y_ = osb[nb % 2], rl[nb % 2], yd[nb % 2]
                    em.copy("dve", o_[0:65, 0:bl], po[0:65, 0:bl])
                    em.op("dve", "reciprocal", out=r_[64:65, 0:bl], in_=o_[64:65, 0:bl])
                    yield
                    pb = self.ps[bb[nb % len(bb)]]
                    em.mm(out=pb[0:64, 0:bl], lhsT=self.cst[64:65, C_ONES:C_ONES + 64], rhs=r_[64:65, 0:bl])
                    em.tt("dve", y_[:, 0:bl], o_[0:64, 0:bl], pb[0:64, 0:bl], ALU.mult)
                    em.dma(out=S["mixT"][768 + h * 64:768 + (h + 1) * 64, b0:b0 + bl], in_=y_[:, 0:bl])
                    nb += 1
                    yield
        return gen()

    def mixer_ssd_mla(self, l):
        em = self.em
        with ExitStack() as stM:
            M = self.mixer_mla(l, outer=stM)
            ga = self.mla_attn(M, stM, (4, 5), (6, 7), (4, 5))
            gs = self.ssd_gen(l, stM, 4)
            while gs is not None or ga is not None:
                if gs is not None:
                    try:
                        next(gs)
                    except StopIteration:
                        gs = None
                for _ in range(1):
                    if ga is not None:
                        try:
                            next(ga)
                        except StopIteration:
                            ga = None
            em.fence()

    def seq_chunks(self, d):
        if d == 0:
            return list(range(NT))
        return [1, 0] + list(range(NT - 1, 1, -1))

    def softplus(self, out, in_, bias_bc, tmp):
        em = self.em
        em.tt("dve", tmp, in_, bias_bc, ALU.add)
        em.act(out=tmp, in_=tmp, func=AF.Exp)
        em.act(out=out, in_=tmp, func=AF.Ln, bias=1.0)

    def mixer_ssd(self, l):
        with ExitStack() as st:
            for _ in self.ssd_gen(l, st, 8):
                pass
            self.em.fence()

    def ssd_gen(self, l, st, nbanks):
        em, nc, S = self.em, self.nc, self.S
        cst = self.cst
        ones = cst[:, C_ONES:C_ONES + 128]
        masks = [cst[:, C_TRIU:C_TRIU + 128], cst[:, C_TRIL:C_TRIL + 128]]
        if True:
            Yacc = self.sbt(st, "sYacc", [128, NT, 256], F32)
            hT = self.sbt(st, "shT", [128, 8, 64], F32)
            hTb = self.sbt(st, "shTb", [128, 8, 64], BF16)
            par = self.sbt(st, "spar", [128, 2, 8], F32)
            dsk = self.sbt(st, "sdsk", [128, 4], F32)
            ngb = self.sbt(st, "sngb", [128, 256], F32)
            em.dma(out=par[:, 0, :], in_=self.I["ssd_dt_bias"][l].partition_broadcast(128))
            em.dma(out=par[:, 1, :], in_=self.I["ssd_a_log"][l].partition_broadcast(128))
            em.act(out=par[:, 1, :], in_=par[:, 1, :], func=AF.Exp)
            em.ts("dve", par[:, 1, :], par[:, 1, :], -1.0, None, ALU.mult)
            em.dma(out=dsk[:], in_=self.I["ssd_d"][l].partition_broadcast(128))
            em.dma(out=ngb[:], in_=self.I["ssd_norm_g"][l].partition_broadcast(128))
            em.op("dve", "memset", ap=hT[:], constant=0.0)
            em.op("dve", "memset", ap=hTb[:], constant=0.0)
            em.op("pool", "memset", ap=Yacc[:], constant=0.0)
            NB = 2
            mk = lambda nm, shp, dt=F32: [[self.sbt(st, "%s%d_%d" % (nm, d, i), shp, dt) for i in range(NB)] for d in range(2)]
            xs, bs = mk("sx", [128, 256], BF16), mk("sb", [128, 256], BF16)
            bts, cts = mk("sbt", [128, 2, 128], BF16), mk("sct", [128, 2, 128], BF16)
            dts, sm, ct = mk("sdt", [128, 8]), mk("ssm", [128, 8, 4]), mk("sctm", [128, 16])
            cbm = [self.sbt(st, "scbm%d" % d, [128, 2, 128], F32) for d in range(2)]
            Wt = [[self.sbt(st, "sw%d_%d" % (d, i), [128, 4, 128], BF16 if i == 2 else F32) for i in range(3)] for d in range(2)]
            Vt = [[self.sbt(st, "sv%d_%d" % (d, i), [128, 4, 64], BF16 if i in (0, 3) else F32) for i in range(5)] for d in range(2)]
            self._pb = 0

            def bank():
                self._pb += 1
                return self.ps[self._pb % nbanks]

            def b4(ap2):
                return ap2.unsqueeze(1).broadcast_to([128, 4, ap2.shape[1]])

            def bh(ap2, n):
                return ap2.unsqueeze(2).broadcast_to([128, 4, n])

            def w3(bk):
                return bk[:, :].rearrange("p (h c) -> p h c", c=128)

            negm = [cst[:, C_NEGU:C_NEGU + 128], cst[:, C_NEGL:C_NEGL + 128]]
            D2, H4 = range(2), range(4)
            seqs = [self.seq_chunks(0), self.seq_chunks(1)]
            def issue_loads(it_):
                i_ = it_ % NB
                for d in D2:
                    r0 = seqs[d][it_] * 128
                    em.dma(out=xs[d][i_][:], in_=S["sd_x"][r0:r0 + 128, :])
                    em.dma(out=bs[d][i_][:], in_=S["sd_B"][r0:r0 + 128, :], q="pool")
                    em.dma(out=bts[d][i_][:], in_=S["sd_BT"][:, r0:r0 + 128].rearrange("(g m) t -> m g t", g=2))
                    em.dma(out=cts[d][i_][:], in_=S["sd_CT"][:, r0:r0 + 128].rearrange("(g m) t -> m g t", g=2), q="pool")
                    em.dma(out=dts[d][i_][:], in_=S["tokM"][r0:r0 + 128, 0:8])

            issue_loads(0)
            for it in range(NT):
                i = it % NB
                cc = [seqs[0][it], seqs[1][it]]
                yield
                if it + 1 < NT:
                    issue_loads(it + 1)
                SM = [sm[d][i] for d in D2]
                CTt = [ct[d][i] for d in D2]
                X3 = [xs[d][i][:].rearrange("p (h e) -> p h e", e=64) for d in D2]
                dsl = [slice(d * 4, d * 4 + 4) for d in D2]
                yield
                for d in D2:
                    em.tt("dve", SM[d][:, 5, :], dts[d][i][:, dsl[d]], par[:, 0, dsl[d]], ALU.add)
                yield
                for d in D2:
                    em.act(out=SM[d][:, 5, :], in_=SM[d][:, 5, :], func=AF.Exp)
                yield
                for d in D2:
                    em.act(out=SM[d][:, 0, :], in_=SM[d][:, 5, :], func=AF.Ln, bias=1.0)
                yield
                for d in D2:
                    em.tt("dve", SM[d][:, 1, :], SM[d][:, 0, :], par[:, 1, dsl[d]], ALU.mult)
                pc = bank()
                yield
                for d in D2:
                    em.mm(out=pc[:, d * 16:d * 16 + 4], lhsT=masks[d], rhs=SM[d][:, 1, :])
                    em.mm(out=pc[:, d * 16 + 8:d * 16 + 12], lhsT=ones, rhs=SM[d][:, 1, :])
                yield
                for d in D2:
                    em.copy("dve", CTt[d][:, 0:12], pc[:, d * 16:d * 16 + 12])
                yield
                for d in D2:
                    em.act(out=SM[d][:, 2, :], in_=CTt[d][:, 0:4], func=AF.Exp)
                    em.tt("dve", SM[d][:, 5, :], CTt[d][:, 8:12], CTt[d][:, 0:4], ALU.subtract)
                yield
                for d in D2:
                    em.act(out=SM[d][:, 3, :], in_=SM[d][:, 5, :], func=AF.Exp)
                    em.act(out=SM[d][:, 4, :], in_=CTt[d][:, 8:12], func=AF.Exp)
                pcb, prb, py, pss = {}, {}, {}, {}
                yield
                for d in D2:
                    em.tt("pool", Wt[d][0][:], b4(masks[d]), bh(SM[d][:, 1, :], 128), ALU.mult)
                    pcb[d] = bank()
                    for g in range(2):
                        em.mm(out=pcb[d][:, g * 128:(g + 1) * 128], lhsT=bts[d][i][:, g, :], rhs=cts[d][i][:, g, :])
                yield
                for d in D2:
                    em.tt("dve", cbm[d][:], pcb[d][:, 0:256].rearrange("p (g c) -> p g c", c=128),
                          masks[d].unsqueeze(1).broadcast_to([128, 2, 128]), ALU.mult)
                yield
                for d in D2:
                    prb[d] = bank()
                    for h in H4:
                        em.mm(out=prb[d][:, h * 128:(h + 1) * 128], lhsT=ones, rhs=Wt[d][0][:, h, :])
                yield
                for d in D2:
                    em.tt("dve", Wt[d][1][:], w3(prb[d]), bh(CTt[d][:, 0:4], 128), ALU.subtract)
                yield
                for d in D2:
                    em.tt("pool", Wt[d][1][:], Wt[d][1][:], b4(negm[d]), ALU.add)
                yield
                for d in D2:
                    em.act(out=Wt[d][1][:], in_=Wt[d][1][:], func=AF.Exp)
                yield
                for d in D2:
                    em.tt("dve", Wt[d][2][:].rearrange("p (g r) c -> p g r c", r=2),
                          Wt[d][1][:].rearrange("p (g r) c -> p g r c", r=2),
                          cbm[d][:].unsqueeze(2).broadcast_to([128, 2, 2, 128]), ALU.mult)
                    em.tt("pool", Vt[d][0][:], X3[d], bh(SM[d][:, 0, :], 64), ALU.mult)
                yield
                for d in D2:
                    em.tt("pool", Vt[d][3][:], Vt[d][0][:], bh(SM[d][:, 3, :], 64), ALU.mult)
                yield
                for d in D2:
                    py[d] = bank()
                    for h in H4:
                        em.mm(out=py[d][:, h * 128:h * 128 + 64], lhsT=Wt[d][2][:, h, :], rhs=Vt[d][0][:, h, :])
                        em.mm(out=py[d][:, h * 128 + 64:(h + 1) * 128], lhsT=cts[d][i][:, h // 2, :], rhs=hTb[:, d * 4 + h, :])
                yield
                for d in D2:
                    py3 = w3(py[d])
                    em.tt("dve", Vt[d][1][:], py3[:, :, 64:128], bh(SM[d][:, 2, :], 64), ALU.mult)
                yield
                for d in D2:
                    py3 = w3(py[d])
                    em.tt("dve", Vt[d][2][:], py3[:, :, 0:64], Vt[d][1][:], ALU.add)
                em.tt("pool", Vt[0][4][:], X3[0], bh(dsk[:], 64), ALU.mult)
                em.tt("pool", Vt[0][2][:], Vt[0][2][:], Vt[0][4][:], ALU.add)
                yield
                for d in D2:
                    ys = Yacc[:, cc[d], :].rearrange("p (h e) -> p h e", e=64)
                    em.tt("pool", ys, ys, Vt[d][2][:], ALU.add)
                yield
                for d in D2:
                    pss[d] = bank()
                    for h in H4:
                        g = h // 2
                        em.mm(out=pss[d][:, h * 64:(h + 1) * 64], lhsT=bs[d][i][:, g * 128:(g + 1) * 128], rhs=Vt[d][3][:, h, :])
                yield
                for d in D2:
                    hs = hT[:, d * 4:(d + 1) * 4, :]
                    em.tt("dve", Vt[d][1][:], hs, bh(SM[d][:, 4, :], 64), ALU.mult)
                yield
                for d in D2:
                    hs = hT[:, d * 4:(d + 1) * 4, :]
                    em.tt("dve", hs, Vt[d][1][:], pss[d][:, 0:256].rearrange("p (h e) -> p h e", e=64), ALU.add)
                yield
                for d in D2:
                    em.copy("act", hTb[:, d * 4:(d + 1) * 4, :], hT[:, d * 4:(d + 1) * 4, :])
            with ExitStack() as st2:
                zs = [self.sbt(st2, "sz%d" % i, [128, 256], F32) for i in range(3)]
                fin = [self.sbt(st2, "sfin%d" % i, [128, 256], F32) for i in range(3)]
                fst = self.sbt(st2, "sfst", [128, 4, NT], F32)
                yT = [self.sbt(st2, "syT%d" % i, [128, 2, 128], BF16) for i in range(3)]
                for c in range(NT):
                    r0 = c * 128
                    z, f, yt = zs[c % 3], fin[c % 3], yT[c % 3]
                    em.dma(out=z[:], in_=S["tokZ"][r0:r0 + 128, :], q=("sp" if c % 2 == 0 else "pool"))
                    em.act(out=z[:], in_=z[:], func=AF.Silu)
                    em.tt("dve", f[:], Yacc[:, c, :], z[:], ALU.mult)
                    em.act(out=z[:], in_=f[:], func=AF.Square, accum_out=fst[:, 0, c:c + 1])
                    em.ts("dve", fst[:, 1, c:c + 1], fst[:, 0, c:c + 1], 1.0 / 256, EPS, ALU.mult, ALU.add)
                    self.rsqrt_cols(fst[:, 2, c:c + 1], fst[:, 1, c:c + 1], fst[:, 3, c:c + 1])
                    em.stt(out=f[:], in0=f[:], scalar=fst[:, 2, c:c + 1], in1=ngb[:], op0=ALU.mult, op1=ALU.mult)
                    pt = bank()
                    for j in range(2):
                        em.tr(out=pt[:, j * 128:(j + 1) * 128], in_=f[:, j * 128:(j + 1) * 128], identity=self.ident)
                    self.evac(yt[:].rearrange("p j t -> p (j t)"), pt[:, 0:256])
                    em.dma(out=S["mixT"][512:768, r0:r0 + 128].rearrange("(j p) t -> p j t", p=128), in_=yt[:])
                    yield

    def cmul(self, e1, e2, out_r, out_i, ar, ai, br, bi, t1, t2):
        em = self.em
        em.tt(e1, t1, ar, br, ALU.mult)
        em.tt(e2, t2, ai, bi, ALU.mult)
        em.tt(e2, out_r, t1, t2, ALU.subtract)
        em.tt(e1, t1, ar, bi, ALU.mult)
        em.tt(e2, t2, ai, br, ALU.mult)
        em.tt(e1, out_i, t1, t2, ALU.add)

    def mixer_s5(self, l):
        em, nc, S = self.em, self.nc, self.S
        cst = self.cst
        PI = float(np.pi)
        with ExitStack() as st:
            Yacc = self.sbt(st, "5Y", [128, 2, T], F32)
            BT = self.sbt(st, "5BT", [128, 16, 2, 128], BF16)
            CZ = self.sbt(st, "5CZ", [128, 16, 2, 128], BF16)
            lam = self.sbt(st, "5lam", [128, 4, 16], F32)
            lpw = self.sbt(st, "5lpw", [128, 4, 10, 16], F32)
            dcol = self.sbt(st, "5dcol", [128, 4], F32)
            Wg = self.sbt(st, "5Wg", [128, 2, 256], BF16)
            ones = self.sbt(st, "5ones", [128, 512], F32)
            em.op("pool", "memset", ap=ones[:], constant=1.0)
            em.dma(out=Wg[:], in_=self.I["s5_w_glu"][l].rearrange("(k p) c -> p k c", p=128), q="pool")
            with ExitStack() as st1:
                t16 = self.sbt(st1, "5t16", [16, 4, 128], F32)
                ls2 = self.sbt(st1, "5ls2", [16, 2], F32)
                pc = self.sbt(st1, "5pc", [128, 12, 16], F32)
                Braw = self.sbt(st1, "5Braw", [128, 2, 16, 16], F32)
                Bbar = self.sbt(st1, "5Bbar", [128, 2, 16, 16], F32)
                bt_ = self.sbt(st1, "5btmp", [128, 2, 16, 16], F32)
                BZ = self.sbt(st1, "5BZ", [128, 16, 2, 128], F32)
                Cd = self.sbt(st1, "5Cd", [128, 2, 4, 64], F32)
                CW = self.sbt(st1, "5CW", [128, 4, 2, 128], F32)
                C2 = self.sbt(st1, "5C2", [128, 4, 2, 128], F32)
                em.dma(out=t16[:, 0, :], in_=self.I["s5_a_re"][l].rearrange("d (p j) n -> (d p) (j n)", j=2))
                em.dma(out=t16[:, 1, :], in_=self.I["s5_a_im"][l].rearrange("d (p j) n -> (d p) (j n)", j=2))
                em.dma(out=ls2[:], in_=self.I["s5_log_step"][l].rearrange("d (p j) -> (d p) j", j=2))
                em.copy("dve", t16[:, 2, :].rearrange("q (j n) -> q j n", j=2), ls2[:].unsqueeze(2).broadcast_to([16, 2, 64]))
                em.dma(out=t16[0:2, 3, :], in_=self.I["s5_d"][l].rearrange("(k p) -> k p", p=128))
                em.dma(out=t16[2:4, 3, :], in_=self.I["s5_b_glu"][l].rearrange("(k p) -> k p", p=128))
                pst = self.ps[0]
                for i in range(3):
                    em.tr(out=pst[:, i * 16:(i + 1) * 16], in_=t16[:, i, :], identity=self.ident[0:16, 0:16])
                em.tr(out=pst[:, 48:52], in_=t16[0:4, 3, :], identity=self.ident[0:4, 0:4])
                em.copy("dve", pc[:, 0:3, :].rearrange("p a b -> p (a b)"), pst[:, 0:48])
                em.copy("dve", dcol[:], pst[:, 48:52])
                are, aim, dl = pc[:, 0, :], pc[:, 1, :], pc[:, 2, :]
                em.act(out=dl, in_=dl, func=AF.Exp)
                em.tt("dve", pc[:, 3, :], aim, dl, ALU.mult)
                em.tt("dve", pc[:, 4, :], are, dl, ALU.mult)
                em.act(out=pc[:, 5, :], in_=pc[:, 4, :], func=AF.Exp)
                em.act(out=pc[:, 6, :], in_=pc[:, 4, :], func=AF.Exp, scale=-1.0)
                sn, cs_ = pc[:, 7, :], pc[:, 8, :]
                em.act(out=sn, in_=pc[:, 3, :], func=AF.Sin, scale=1.0 / 16)
                em.ts("dve", pc[:, 9, :], pc[:, 3, :], 1.0 / 16, PI / 2, ALU.mult, ALU.add)
                em.act(out=cs_, in_=pc[:, 9, :], func=AF.Sin)
                for _ in range(4):
                    em.tt("dve", pc[:, 9, :], cs_, cs_, ALU.mult)
                    em.tt("dve", pc[:, 10, :], sn, sn, ALU.mult)
                    em.tt("dve", pc[:, 11, :], cs_, sn, ALU.mult)
                    em.tt("dve", cs_, pc[:, 9, :], pc[:, 10, :], ALU.subtract)
                    em.ts("dve", sn, pc[:, 11, :], 2.0, None, ALU.mult)
                em.tt("dve", lam[:, 0, :], pc[:, 5, :], cs_, ALU.mult)
                em.tt("dve", lam[:, 1, :], pc[:, 5, :], sn, ALU.mult)
                em.tt("dve", lam[:, 2, :], pc[:, 6, :], cs_, ALU.mult)
                em.stt(out=lam[:, 3, :], in0=pc[:, 6, :], scalar=-1.0, in1=sn, op0=ALU.mult, op1=ALU.mult)
                for b in range(2):
                    em.copy("dve", lpw[:, 2 * b, 0, :], lam[:, 2 * b, :])
                    em.copy("dve", lpw[:, 2 * b + 1, 0, :], lam[:, 2 * b + 1, :])
                    for j in range(1, 10):
                        pr, pi_ = lpw[:, 2 * b, j - 1, :], lpw[:, 2 * b + 1, j - 1, :]
                        em.tt("dve", pc[:, 9, :], pr, pr, ALU.mult)
                        em.tt("dve", pc[:, 10, :], pi_, pi_, ALU.mult)
                        em.tt("dve", pc[:, 11, :], pr, pi_, ALU.mult)
                        em.tt("dve", lpw[:, 2 * b, j, :], pc[:, 9, :], pc[:, 10, :], ALU.subtract)
                        em.ts("dve", lpw[:, 2 * b + 1, j, :], pc[:, 11, :], 2.0, None, ALU.mult)
                nr, ni = pc[:, 5, :], lam[:, 1, :]
                em.ts("dve", nr, lam[:, 0, :], -1.0, None, ALU.add)
                em.tt("dve", pc[:, 9, :], are, are, ALU.mult)
                em.tt("dve", pc[:, 10, :], aim, aim, ALU.mult)
                em.tt("dve", pc[:, 9, :], pc[:, 9, :], pc[:, 10, :], ALU.add)
                em.op("dve", "reciprocal", out=pc[:, 6, :], in_=pc[:, 9, :])
                em.tt("dve", pc[:, 9, :], nr, are, ALU.mult)
                em.tt("dve", pc[:, 10, :], ni, aim, ALU.mult)
                em.tt("dve", pc[:, 9, :], pc[:, 9, :], pc[:, 10, :], ALU.add)
                em.tt("dve", pc[:, 7, :], pc[:, 9, :], pc[:, 6, :], ALU.mult)
                em.tt("dve", pc[:, 9, :], ni, are, ALU.mult)
                em.tt("dve", pc[:, 10, :], nr, aim, ALU.mult)
                em.tt("dve", pc[:, 9, :], pc[:, 9, :], pc[:, 10, :], ALU.subtract)
                em.tt("dve", pc[:, 8, :], pc[:, 9, :], pc[:, 6, :], ALU.mult)
                em.dma(out=Braw[:, 0], in_=self.I["s5_b_re"][l].rearrange("d (p j) n i -> (j n) (d p) i", j=2))
                em.dma(out=Braw[:, 1], in_=self.I["s5_b_im"][l].rearrange("d (p j) n i -> (j n) (d p) i", j=2))
                cr3 = pc[:, 7, :].unsqueeze(2).broadcast_to([128, 16, 16])
                ci3 = pc[:, 8, :].unsqueeze(2).broadcast_to([128, 16, 16])
                self.cmul("dve", "pool", Bbar[:, 0], Bbar[:, 1], Braw[:, 0], Braw[:, 1], cr3, ci3, bt_[:, 0], bt_[:, 1])
                em.op("pool", "memset", ap=BZ[:], constant=0.0)
                for q in range(4):
                    for j in range(2):
                        for ri in range(2):
                            em.copy("dve" if ri == 0 else "pool",
                                    BZ[j * 64:(j + 1) * 64, q::4, ri, q * 32 + j * 16:q * 32 + j * 16 + 16],
                                    Bbar[j * 64:(j + 1) * 64, ri, q::4, :])
                for dp in range(16):
                    pst = self.ps[dp % 4]
                    for ri in range(2):
                        em.tr(out=pst[:, ri * 128:(ri + 1) * 128], in_=BZ[:, dp, ri, :], identity=self.ident)
                    self.evac(BT[:, dp, :, :].rearrange("p r c -> p (r c)"), pst[:, 0:256])
                em.dma(out=Cd[:, 0], in_=self.I["s5_c_re"][l].rearrange("d (gh g8) o n -> (g8 o) (d gh) n", gh=2))
                em.dma(out=Cd[:, 1], in_=self.I["s5_c_im"][l].rearrange("d (gh g8) o n -> (g8 o) (d gh) n", gh=2))
                for ri in range(2):
                    sgn = 1.0 if ri == 0 else -1.0
                    em.ts("dve", CW[:, :, ri, 0:64], Cd[:, ri], cst[:, C_EO:C_EO + 1], sgn, ALU.mult, ALU.mult)
                    em.ts("dve", CW[:, :, ri, 64:128], Cd[:, ri], cst[:, C_EO + 1:C_EO + 2], sgn, ALU.mult, ALU.mult)
                for a in range(4):
                    pst = self.ps[4 + a % 2]
                    for ri in range(2):
                        em.tr(out=pst[:, ri * 128:(ri + 1) * 128], in_=CW[:, a, ri, :], identity=self.ident)
                    self.evac(C2[:, a, :, :].rearrange("p r c -> p (r c)"), pst[:, 0:256])
                em.op("pool", "memset", ap=CZ[:], constant=0.0)
                for q in range(4):
                    em.copy("dve", CZ[:].rearrange("p (a q) r c -> p a q r c", q=4)[:, :, q, :, q * 32:(q + 1) * 32],
                            C2[:, :, :, q * 32:(q + 1) * 32])
                em.fence()
            with ExitStack() as st2:
                TBf = self.sbt(st2, "5TBf", [128, 4, 4, 512], F32)
                TBs = [self.sbt(st2, "5TB%d" % i, [128, 4, 4, 512], BF16) for i in range(2)]
                uT = [self.sbt(st2, "5u%d" % i, [128, 512], BF16) for i in range(2)]
                gb = [self.sbt(st2, "5g%d" % i, [128, 2, 512], BF16) for i in range(4)]
                Gb = [self.sbt(st2, "5G%d" % i, [128, 2, 512], BF16) for i in range(4)]
                hb = [self.sbt(st2, "5h%d" % i, [128, 2, 512], BF16) for i in range(4)]
                bub = [self.sbt(st2, "5bu%d" % i, [128, 2, 512], BF16) for i in range(4)]
                tb = [self.sbt(st2, "5t%d" % i, [128, 2, 512], BF16) for i in range(4)]
                init = self.sbt(st2, "5init", [128, 4, 4], F32)
                tt1 = self.sbt(st2, "5tt1", [128, 2, 4, 256], F32)
                tt2 = self.sbt(st2, "5tt2", [128, 2, 4, 256], F32)

                def table_gen(d_, gh_, TBo, engs=("dve", "pool")):
                    dps = slice(d_ * 8 + gh_ * 4, d_ * 8 + gh_ * 4 + 4)
                    em.op("pool", "memset", ap=TBf[:, :, 0, 0:1], constant=1.0)
                    em.op("pool", "memset", ap=TBf[:, :, 1, 0:1], constant=0.0)
                    em.op("pool", "memset", ap=TBf[:, :, 2, 0:1], constant=1.0)
                    em.op("pool", "memset", ap=TBf[:, :, 3, 0:1], constant=0.0)
                    yield
                    for j in range(9):
                        cnt = 1 << j
                        steps = []
                        for b in range(2):
                            Lr = lpw[:, 2 * b, j, dps].unsqueeze(2).broadcast_to([128, 4, cnt])
                            Li = lpw[:, 2 * b + 1, j, dps].unsqueeze(2).broadcast_to([128, 4, cnt])
                            ar, ai = TBf[:, :, 2 * b, 0:cnt], TBf[:, :, 2 * b + 1, 0:cnt]
                            o_r, o_i = TBf[:, :, 2 * b, cnt:2 * cnt], TBf[:, :, 2 * b + 1, cnt:2 * cnt]
                            t1, t2 = tt1[:, b, :, 0:cnt], tt2[:, b, :, 0:cnt]
                            steps.append([(t1, ar, Lr, ALU.mult), (t2, ai, Li, ALU.mult), (o_r, t1, t2, ALU.subtract),
                                          (t1, ar, Li, ALU.mult), (t2, ai, Lr, ALU.mult), (o_i, t1, t2, ALU.add)])
                        for k in range(6):
                            for b in range(2):
                                o, a, bb_, op = steps[b][k]
                                em.tt(engs[b], o, a, bb_, op)
                            yield
                    em.copy("act", TBo[:].rearrange("p a b c -> p (a b c)"), TBf[:].rearrange("p a b c -> p (a b c)"))
                    yield

                combos = [(0, 0), (0, 1), (1, 0), (1, 1)]
                for _ in table_gen(0, 0, TBs[0], ("dve", "pool")):
                    pass
                n = 0
                for ci_, (d, gh) in enumerate(combos):
                    if True:
                        TB = TBs[ci_ % 2]
                        nxt = table_gen(combos[ci_ + 1][0], combos[ci_ + 1][1], TBs[(ci_ + 1) % 2]) if ci_ + 1 < 4 else None
                        blocks = BLOCKS if d == 0 else [BLOCKS[0]] + BLOCKS[:0:-1]
                        for bi, (b0, bl) in enumerate(blocks):
                            u = uT[n % 2]
                            em.dma(out=u[:, 0:bl], in_=S["s5_uT"][gh * 128:(gh + 1) * 128, b0:b0 + bl])
                            py = self.ps[6 + n % 2]
                            n += 1

                            def sv(ap3, a):
                                return ap3[:, a, 0:bl] if d == 0 else ap3[:, a, bl - 1::-1] if bl == 512 else ap3[:, a, bl - 1::-1]

                            QS = range(4)
                            dpq = [d * 8 + gh * 4 + q for q in QS]

                            def tv(ap3, a):
                                return ap3[:, a, 0:bl] if d == 0 else ap3[:, a, bl - 1::-1]

                            def sv(ap3, a):
                                return ap3[:, a, 0:bl] if d == 0 else ap3[:, a, bl - 1::-1]

                            for q in QS:
                                for ri in range(2):
                                    self._pb += 1
                                    pb_ = self.ps[self._pb % 6]
                                    em.mm(out=pb_[:, 0:bl], lhsT=BT[:, dpq[q], ri, :], rhs=u[:, 0:bl])
                                    em.copy("act", bub[q][:, ri, 0:bl], pb_[:, 0:bl])

                            def cmul_steps(o, a, tbl, ia, ib, t_):
                                ar, ai = a[:, 0, 0:bl], a[:, 1, 0:bl]
                                br, bi_ = tv(tbl, ia), tv(tbl, ib)
                                t1, t2 = t_[:, 0, 0:bl], t_[:, 1, 0:bl]
                                return [lambda: em.tt("dve", t1, ar, br, ALU.mult),
                                        lambda: em.tt("dve", t2, ai, bi_, ALU.mult),
                                        lambda: em.tt("dve", o[:, 0, 0:bl], t1, t2, ALU.subtract),
                                        lambda: em.tt("dve", t1, ar, bi_, ALU.mult),
                                        lambda: em.tt("dve", t2, ai, br, ALU.mult),
                                        lambda: em.tt("dve", o[:, 1, 0:bl], t1, t2, ALU.add)]

                            st_ = [cmul_steps(gb[q], bub[q], TB[:, q], 2, 3, tb[q]) for q in QS]
                            for k in range(6):
                                for q in QS:
                                    st_[q][k]()
                            for ri in range(2):
                                for q in QS:
                                    ini = 0.0 if bi == 0 else init[:, q, ri:ri + 1]
                                    em.op("dve", "tensor_tensor_scan", out=sv(Gb[q], ri), data0=ones[:, 0:bl],
                                          data1=sv(gb[q], ri), initial=ini, op0=ALU.mult, op1=ALU.add)
                            st_ = [cmul_steps(hb[q], Gb[q], TB[:, q], 0, 1, tb[q]) for q in QS]
                            for k in range(6):
                                for q in QS:
                                    st_[q][k]()
                            e = bl - 1 if d == 0 else 0
                            for q in QS:
                                em.ts("dve", init[:, q, 2:3], hb[q][:, 1, e:e + 1], lam[:, 1, dpq[q]:dpq[q] + 1], None, ALU.mult)
                            for q in QS:
                                em.ts("dve", init[:, q, 3:4], hb[q][:, 0, e:e + 1], lam[:, 1, dpq[q]:dpq[q] + 1], None, ALU.mult)
                            for q in QS:
                                em.stt(out=init[:, q, 0:1], in0=hb[q][:, 0, e:e + 1], scalar=lam[:, 0, dpq[q]:dpq[q] + 1],
                                       in1=init[:, q, 2:3], op0=ALU.mult, op1=ALU.subtract)
                            for q in QS:
                                em.stt(out=init[:, q, 1:2], in0=hb[q][:, 1, e:e + 1], scalar=lam[:, 0, dpq[q]:dpq[q] + 1],
                                       in1=init[:, q, 3:4], op0=ALU.mult, op1=ALU.add)
                            for q in QS:
                                em.mm(out=py[:, 0:bl], lhsT=CZ[:, dpq[q], 0, :], rhs=hb[q][:, 0, 0:bl], start=(q == 0), stop=False)
                                em.mm(out=py[:, 0:bl], lhsT=CZ[:, dpq[q], 1, :], rhs=hb[q][:, 1, 0:bl], start=False, stop=(q == 3))
                            if d == 0:
                                em.copy("act", Yacc[:, gh, b0:b0 + bl], py[:, 0:bl])
                            else:
                                em.tt("dve", Yacc[:, gh, b0:b0 + bl], Yacc[:, gh, b0:b0 + bl], py[:, 0:bl], ALU.add)
                            nxt = self._adv(nxt, 8)
                        self._adv(nxt, 10 ** 6)
                em.fence()
            with ExitStack() as st3:
                uu = [self.sbt(st3, "5uu%d" % i, [128, 2, 512], BF16) for i in range(2)]
                vv = [self.sbt(st3, "5vv%d" % i, [128, 2, 512], F32) for i in range(2)]
                vb = [self.sbt(st3, "5vb%d" % i, [128, 2, 512], BF16) for i in range(2)]
                w1 = self.sbt(st3, "5w1", [128, 2, 512], F32)
                ob = [self.sbt(st3, "5ob%d" % i, [128, 2, 512], BF16) for i in range(2)]
                K2 = 2.0 * float(np.sqrt(2.0 / np.pi))
                for bi, (b0, bl) in enumerate(BLOCKS):
                    u, v, vbf, o = uu[bi % 2], vv[bi % 2], vb[bi % 2], ob[bi % 2]
                    em.dma(out=u[:, :, 0:bl], in_=S["s5_uT"][:, b0:b0 + bl].rearrange("(c p) t -> p c t", p=128))
                    for c in range(2):
                        em.stt(out=v[:, c, 0:bl], in0=u[:, c, 0:bl], scalar=dcol[:, c:c + 1], in1=Yacc[:, c, b0:b0 + bl],
                               op0=ALU.mult, op1=ALU.add)
                    x = v[:, :, 0:bl]
                    em.tt("pool", w1[:, :, 0:bl], x, x, ALU.mult)
                    em.ts("dve", w1[:, :, 0:bl], w1[:, :, 0:bl], 0.044715, 1.0, ALU.mult, ALU.add)
                    em.tt("pool", w1[:, :, 0:bl], w1[:, :, 0:bl], x, ALU.mult)
                    em.act(out=w1[:, :, 0:bl], in_=w1[:, :, 0:bl], func=AF.Sigmoid, scale=K2)
                    em.tt("dve", x, x, w1[:, :, 0:bl], ALU.mult)
                    em.copy("pool", vbf[:, :, 0:bl], x)
                    for nc_ in range(2):
                        pz = self.ps[4 + nc_]
                        for kc in range(2):
                            em.mm(out=pz[:, 0:bl], lhsT=Wg[:, kc, nc_ * 128:(nc_ + 1) * 128], rhs=vbf[:, kc, 0:bl],
                                  start=(kc == 0), stop=(kc == 1))
                        em.act(out=w1[:, nc_, 0:bl], in_=pz[:, 0:bl], func=AF.Sigmoid, bias=dcol[:, 2 + nc_:3 + nc_])
                    em.tt("dve", o[:, :, 0:bl], x, w1[:, :, 0:bl], ALU.mult)
                    em.dma(out=S["mixT"][256:512, b0:b0 + bl].rearrange("(c p) t -> p c t", p=128), in_=o[:, :, 0:bl])
            em.fence()

    def mixer_gdn(self, l):
        em, nc, S = self.em, self.nc, self.S
        cst = self.cst
        ones = cst[:, C_ONES:C_ONES + 128]
        ident = self.ident
        identb = self.identb[:]

        def b4(ap2):
            return ap2.unsqueeze(1).broadcast_to([ap2.shape[0], 4, ap2.shape[1]])

        def bh(ap2, n, p=128):
            return ap2.unsqueeze(2).broadcast_to([p, 4, n])

        offd4 = b4(cst[:, C_OFFD:C_OFFD + 128])
        bd16_4 = b4(cst[:, C_BD16:C_BD16 + 128])
        mm4 = [b4(cst[:, C_M1:C_M1 + 128]), b4(cst[:, C_M2:C_M2 + 128]), b4(cst[:, C_M3:C_M3 + 128])]
        masks = [cst[:, C_TRIU:C_TRIU + 128], cst[:, C_TRIL:C_TRIL + 128]]
        negm = [cst[:, C_NEGU:C_NEGU + 128], cst[:, C_NEGL:C_NEGL + 128]]
        id4 = b4(ident)
        with ExitStack() as st:
            Oacc = self.sbt(st, "gO", [128, NT, 256], F32)
            Sst = self.sbt(st, "gS", [64, 8, 64], F32)
            Sstb = self.sbt(st, "gSb", [64, 8, 64], BF16)
            par = self.sbt(st, "gpar", [128, 2, 8], F32)
            ngb = self.sbt(st, "gngb", [128, 64], F32)
            em.dma(out=par[:, 0, :], in_=self.I["gdn_dt_bias"][l].partition_broadcast(128))
            em.dma(out=par[:, 1, :], in_=self.I["gdn_a_log"][l].partition_broadcast(128))
            em.act(out=par[:, 1, :], in_=par[:, 1, :], func=AF.Exp)
            em.ts("dve", par[:, 1, :], par[:, 1, :], -1.0, None, ALU.mult)
            em.dma(out=ngb[:], in_=self.I["gdn_norm_g"][l].partition_broadcast(128))
            em.op("dve", "memset", ap=Sst[:], constant=0.0)
            em.op("dve", "memset", ap=Sstb[:], constant=0.0)
            em.op("pool", "memset", ap=Oacc[:], constant=0.0)
            NB = 2
            mk = lambda nm, shp, dt=F32: [[self.sbt(st, "%s%d_%d" % (nm, d, i), shp, dt) for i in range(NB)] for d in range(2)]
            qTs, kTs = mk("gq", [64, 4, 128], BF16), mk("gk", [64, 4, 128], BF16)
            kts, vts, tas = mk("gkt", [128, 4, 64], BF16), mk("gvt", [128, 4, 64], BF16), mk("gta", [128, 16])
            sm, ct = mk("gsm", [128, 8, 4]), mk("gct", [128, 16])
            W = [[self.sbt(st, "gw%d_%d" % (d, i), [128, 4, 128], F32 if i in (0, 2, 3) else BF16) for i in range(19)]
                 for d in range(2)]
            V = [[self.sbt(st, "gv%d_%d" % (d, i), [128, 4, 64], F32 if i in (2, 4) else BF16) for i in range(6)]
                 for d in range(2)]
            wTs = [self.sbt(st, "gwT%d" % d, [64, 4, 128], BF16) for d in range(2)]
            stmp = [self.sbt(st, "gstmp%d" % d, [64, 4, 64], F32) for d in range(2)]
            self._pb = 0

            def bank():
                self._pb += 1
                return self.ps[self._pb % 8]

            def w3(bk):
                return bk[:, :].rearrange("p (h c) -> p h c", c=128)

            def w3b(bk):
                return bk[:].bitcast(BF16)[:, 0:512].rearrange("p (h c) -> p h c", c=128)

            D2 = range(2)
            H4 = range(4)
            seqs = [self.seq_chunks(0), self.seq_chunks(1)]
            def issue_loads(it_):
                i_ = it_ % NB
                for d in D2:
                    r0 = seqs[d][it_] * 128
                    em.dma(out=qTs[d][i_][:], in_=S["gd_qT"][:, r0:r0 + 128].rearrange("(h d) t -> d h t", h=4))
                    em.dma(out=kTs[d][i_][:], in_=S["gd_kT"][:, r0:r0 + 128].rearrange("(h d) t -> d h t", h=4), q="pool")
                    em.dma(out=kts[d][i_][:].rearrange("p h e -> p (h e)"), in_=S["gd_k"][r0:r0 + 128, :])
                    em.dma(out=vts[d][i_][:].rearrange("p h e -> p (h e)"), in_=S["gd_v"][r0:r0 + 128, :], q="pool")
                    em.dma(out=tas[d][i_][:], in_=S["tokA"][r0:r0 + 128, 256:272])

            issue_loads(0)
            for it in range(NT):
                i = it % NB
                cc = [seqs[0][it], seqs[1][it]]
                if it + 1 < NT:
                    issue_loads(it + 1)
                SM = [sm[d][i] for d in D2]
                CT = [ct[d][i] for d in D2]
                TA = [tas[d][i] for d in D2]
                dsl = [slice(d * 4, d * 4 + 4) for d in D2]
                for d in D2:
                    em.tt("dve", SM[d][:, 5, :], TA[d][:, d * 4:d * 4 + 4], par[:, 0, dsl[d]], ALU.add)
                for d in D2:
                    em.act(out=SM[d][:, 5, :], in_=SM[d][:, 5, :], func=AF.Exp)
                for d in D2:
                    em.act(out=SM[d][:, 0, :], in_=SM[d][:, 5, :], func=AF.Ln, bias=1.0)
                for d in D2:
                    em.tt("dve", SM[d][:, 0, :], SM[d][:, 0, :], par[:, 1, dsl[d]], ALU.mult)
                for d in D2:
                    em.act(out=SM[d][:, 1, :], in_=TA[d][:, 8 + d * 4:12 + d * 4], func=AF.Sigmoid)
                pc = bank()
                for d in D2:
                    em.mm(out=pc[:, d * 16:d * 16 + 4], lhsT=masks[d], rhs=SM[d][:, 0, :])
                    em.mm(out=pc[:, d * 16 + 8:d * 16 + 12], lhsT=ones, rhs=SM[d][:, 0, :])
                for d in D2:
                    em.copy("dve", CT[d][:, 0:12], pc[:, d * 16:d * 16 + 12])
                for d in D2:
                    em.act(out=SM[d][:, 2, :], in_=CT[d][:, 0:4], func=AF.Exp)
                    em.tt("dve", SM[d][:, 5, :], CT[d][:, 8:12], CT[d][:, 0:4], ALU.subtract)
                for d in D2:
                    em.act(out=SM[d][:, 3, :], in_=SM[d][:, 5, :], func=AF.Exp)
                    em.act(out=SM[d][:, 4, :], in_=CT[d][:, 8:12], func=AF.Exp)
                    em.tt("dve", SM[d][:, 6, :], SM[d][:, 1, :], SM[d][:, 2, :], ALU.mult)
                pk, prb, pq = {}, {}, {}
                for d in D2:
                    em.tt("pool", W[d][3][:], b4(masks[d]), bh(SM[d][:, 0, :], 128), ALU.mult)
                    pk[d] = bank()
                    for h in H4:
                        em.mm(out=pk[d][:, h * 128:(h + 1) * 128], lhsT=kTs[d][i][:, h, :], rhs=kTs[d][i][:, h, :])
                for d in D2:
                    em.tt("dve", W[d][0][:], w3(pk[d]), offd4, ALU.mult)
                for d in D2:
                    prb[d] = bank()
                    for h in H4:
                        em.mm(out=prb[d][:, h * 128:(h + 1) * 128], lhsT=ones, rhs=W[d][3][:, h, :])
                for d in D2:
                    em.tt("dve", W[d][2][:], w3(prb[d]), bh(CT[d][:, 0:4], 128), ALU.subtract)
                for d in D2:
                    em.tt("pool", W[d][2][:], W[d][2][:], b4(negm[d]), ALU.add)
                for d in D2:
                    em.act(out=W[d][2][:], in_=W[d][2][:], func=AF.Exp)
                for d in D2:
                    pq[d] = bank()
                    for h in H4:
                        em.mm(out=pq[d][:, h * 128:(h + 1) * 128], lhsT=kTs[d][i][:, h, :], rhs=qTs[d][i][:, h, :])
                    em.tt("pool", W[d][3][:], W[d][0][:], W[d][2][:], ALU.mult)
                for d in D2:
                    em.tt("dve", W[d][1][:], w3(pq[d]), W[d][2][:], ALU.mult)
                pa, pb = {}, {}
                for d in D2:
                    pa[d] = bank()
                    for h in H4:
                        em.tr(out=pa[d][:, h * 128:(h + 1) * 128], in_=W[d][3][:, h, :], identity=ident)
                for d in D2:
                    em.tt("dve", W[d][4][:], w3(pa[d]), bh(SM[d][:, 1, :], 128), ALU.mult)
                for d in D2:
                    em.tt("pool", W[d][6][:], W[d][4][:], bd16_4, ALU.mult)
                    pb[d] = bank()
                    for h in H4:
                        em.tr(out=w3b(pb[d])[:, h, :], in_=W[d][4][:, h, :], identity=identb)
                for d in D2:
                    em.tt("dve", W[d][7][:], w3b(pb[d]), bd16_4, ALU.mult)
                    for m in range(3):
                        em.tt("pool", W[d][16 + m][:], W[d][4][:], mm4[m], ALU.mult)
                for d in D2:
                    em.stt(out=W[d][12][:], in0=W[d][7][:], scalar=-1.0, in1=id4, op0=ALU.mult, op1=ALU.add)
                P = [W[d][7] for d in D2]
                PT = [W[d][6] for d in D2]
                R = [W[d][12] for d in D2]
                for j in range(1, 4):
                    nP, nPT, nR = [None, None], [None, None], [None, None]
                    p1, p2, p3 = {}, {}, {}
                    for d in D2:
                        p1[d] = bank()
                        for h in H4:
                            em.mm(out=p1[d][:, h * 128:(h + 1) * 128], lhsT=P[d][:, h, :], rhs=PT[d][:, h, :])
                    for d in D2:
                        nPT[d] = W[d][10 + j % 2]
                        em.copy("act" if d == 0 else "dve", nPT[d][:], w3(p1[d]))
                    if j < 3:
                        for d in D2:
                            p2[d] = bank()
                            for h in H4:
                                em.mm(out=p2[d][:, h * 128:(h + 1) * 128], lhsT=PT[d][:, h, :], rhs=P[d][:, h, :])
                        for d in D2:
                            nP[d] = W[d][8 + j % 2]
                            em.copy("dve" if d == 0 else "act", nP[d][:], w3(p2[d]))
                    for d in D2:
                        p3[d] = bank()
                        for h in H4:
                            o_ = p3[d][:, h * 128:(h + 1) * 128]
                            em.mm(out=o_, lhsT=nPT[d][:, h, :], rhs=R[d][:, h, :], start=True, stop=False)
                            em.mm(out=o_, lhsT=identb, rhs=R[d][:, h, :], start=False, stop=True)
                    for d in D2:
                        nR[d] = W[d][12 + j % 2]
                        em.copy("act" if d == 0 else "dve", nR[d][:], w3(p3[d]))
                    P, PT, R = (nP if j < 3 else P), nPT, nR
                for m in range(3):
                    nR = [None, None]
                    px, p1, p2 = {}, {}, {}
                    for d in D2:
                        px[d] = bank()
                        for h in H4:
                            em.tr(out=w3b(px[d])[:, h, :], in_=R[d][:, h, :], identity=identb)
                    for d in D2:
                        em.copy("act" if d == 0 else "dve", W[d][14][:], w3b(px[d]))
                    for d in D2:
                        p1[d] = bank()
                        for h in H4:
                            em.mm(out=p1[d][:, h * 128:(h + 1) * 128], lhsT=W[d][16 + m][:, h, :], rhs=R[d][:, h, :])
                    for d in D2:
                        if d == 0:
                            em.ts("dve", W[d][15][:], w3(p1[d]), -1.0, None, ALU.mult)
                        else:
                            em.act(out=W[d][15][:], in_=w3(p1[d]), func=AF.Copy, scale=-1.0)
                    for d in D2:
                        p2[d] = bank()
                        for h in H4:
                            o_ = p2[d][:, h * 128:(h + 1) * 128]
                            em.mm(out=o_, lhsT=W[d][14][:, h, :], rhs=W[d][15][:, h, :], start=True, stop=False)
                            em.mm(out=o_, lhsT=identb, rhs=R[d][:, h, :], start=False, stop=True)
                    for d in D2:
                        nR[d] = W[d][13] if R[d] is W[d][12] else W[d][12]
                        em.copy("dve" if d == 0 else "act", nR[d][:], w3(p2[d]))
                    R = nR
                pu, pw = {}, {}
                for d in D2:
                    em.tt("pool", V[d][0][:], vts[d][i][:], bh(SM[d][:, 1, :], 64), ALU.mult)
                    em.tt("dve", V[d][1][:], kts[d][i][:], bh(SM[d][:, 6, :], 64), ALU.mult)
                    em.tt("pool", V[d][5][:], kts[d][i][:], bh(SM[d][:, 3, :], 64), ALU.mult)
                for d in D2:
                    pu[d] = bank()
                    pw[d] = bank()
                    for h in H4:
                        em.mm(out=pu[d][:, h * 64:(h + 1) * 64], lhsT=R[d][:, h, :], rhs=V[d][0][:, h, :])
                        em.mm(out=pw[d][0:64, h * 128:(h + 1) * 128], lhsT=V[d][1][:, h, :], rhs=R[d][:, h, :])
                for d in D2:
                    em.copy("act", V[d][2][:], pu[d][:, 0:256].rearrange("p (h e) -> p h e", e=64))
                    em.copy("dve", wTs[d][:], pw[d][0:64, :].rearrange("p (h c) -> p h c", c=128))
                pv, po = {}, {}
                for d in D2:
                    pv[d] = bank()
                    for h in H4:
                        em.mm(out=pv[d][:, h * 128:h * 128 + 64], lhsT=wTs[d][:, h, :], rhs=Sstb[:, d * 4 + h, :])
                        em.mm(out=pv[d][:, h * 128 + 64:(h + 1) * 128], lhsT=qTs[d][i][:, h, :], rhs=Sstb[:, d * 4 + h, :])
                for d in D2:
                    pv3 = w3(pv[d])
                    em.tt("dve", V[d][3][:], V[d][2][:], pv3[:, :, 0:64], ALU.subtract)
                    em.tt("dve", V[d][4][:], pv3[:, :, 64:128], bh(SM[d][:, 2, :], 64), ALU.mult)
                for d in D2:
                    po[d] = bank()
                    for h in H4:
                        em.mm(out=po[d][:, h * 128:h * 128 + 64], lhsT=W[d][1][:, h, :], rhs=V[d][3][:, h, :])
                        em.mm(out=po[d][0:64, h * 128 + 64:(h + 1) * 128], lhsT=V[d][5][:, h, :], rhs=V[d][3][:, h, :])
                for d in D2:
                    po3 = w3(po[d])
                    em.tt("dve", V[d][4][:], po3[:, :, 0:64], V[d][4][:], ALU.add)
                    em.tt("pool", stmp[d][:], Sst[:, d * 4:(d + 1) * 4, :], bh(SM[d][0:64, 4, :], 64, 64), ALU.mult)
                for d in D2:
                    po3 = w3(po[d])
                    em.tt("dve", Sst[:, d * 4:(d + 1) * 4, :], stmp[d][:], po3[0:64, :, 64:128], ALU.add)
                    os_ = Oacc[:, cc[d], :].rearrange("p (h e) -> p h e", e=64)
                    em.tt("pool", os_, os_, V[d][4][:], ALU.add)
                for d in D2:
                    em.copy("act", Sstb[:, d * 4:(d + 1) * 4, :], Sst[:, d * 4:(d + 1) * 4, :])
            with ExitStack() as st2:
                NSF = 4
                fin = [self.sbt(st2, "gfin%d" % i, [128, 256], F32) for i in range(NSF)]
                fsq = [self.sbt(st2, "gfsq%d" % i, [128, 256], F32) for i in range(NSF)]
                gt = [self.sbt(st2, "ggt%d" % i, [128, 256], F32) for i in range(NSF)]
                fst = [self.sbt(st2, "gfst%d" % i, [128, 4, 4], F32) for i in range(NSF)]
                yT = [self.sbt(st2, "gyT%d" % i, [128, 2, 128], BF16) for i in range(NSF)]

                def fin_chunk(c, k):
                    r0 = c * 128
                    f, q_, g_, s_, yt = fin[k], fsq[k], gt[k], fst[k], yT[k]
                    em.dma(out=g_[:], in_=S["tokA"][r0:r0 + 128, 0:256], q=("sp" if c % 2 == 0 else "pool"))
                    o3 = Oacc[:, c, :].rearrange("p (h e) -> p h e", e=64)
                    q3_ = q_[:].rearrange("p (h e) -> p h e", e=64)
                    em.tt("pool", q3_, o3, o3, ALU.mult)
                    yield
                    em.op("dve", "tensor_reduce", out=s_[:, 0, :], in_=q3_, axis=AX.X, op=ALU.add)
                    em.act(out=g_[:], in_=g_[:], func=AF.Silu)
                    yield
                    em.ts("dve", s_[:, 1, :], s_[:, 0, :], 1.0 / 64, EPS, ALU.mult, ALU.add)
                    yield
                    em.act(out=s_[:, 3, :], in_=s_[:, 1, :], func=AF.Sqrt)
                    yield
                    em.op("dve", "reciprocal", out=s_[:, 2, :], in_=s_[:, 3, :])
                    yield
                    f3 = f[:].rearrange("p (h e) -> p h e", e=64)
                    em.tt("dve", f3, o3, s_[:, 2, :].unsqueeze(2).broadcast_to([128, 4, 64]), ALU.mult)
                    yield
                    em.tt("pool", f3, f3, ngb[:].unsqueeze(1).broadcast_to([128, 4, 64]), ALU.mult)
                    yield
                    em.tt("dve", f[:], f[:], g_[:], ALU.mult)
                    yield
                    pt = self.ps[2 * k]
                    for j in range(2):
                        em.tr(out=pt[:, j * 128:(j + 1) * 128], in_=f[:, j * 128:(j + 1) * 128], identity=ident)
                    yield
                    self.evac(yt[:].rearrange("p j t -> p (j t)"), pt[:, 0:256])
                    yield
                    em.dma(out=S["mixT"][0:256, r0:r0 + 128].rearrange("(j p) t -> p j t", p=128), in_=yt[:])
                    yield

                pending, active, free = list(range(NT)), [], list(range(NSF))
                while pending or active:
                    while pending and free:
                        k = free.pop(0)
                        active.append((fin_chunk(pending.pop(0), k), k))
                    for item in list(active):
                        try:
                            next(item[0])
                        except StopIteration:
                            active.remove(item)
                            free.append(item[1])
            em.fence()

    def phaseC(self, l, mix_src=None):
        em, nc, S = self.em, self.nc, self.S
        hsrc = self.I["hcat"] if l == 0 else S["hbuf"]
        mixT = S["mixT"] if mix_src is None else mix_src
        last = (l == self.nlayers - 1)
        blocks = BLOCKS[1:] if last else BLOCKS
        with ExitStack() as st:
            W1 = self.sbt(st, "W1", [128, 8, 2048], BF16)
            W2 = self.sbt(st, "W2", [128, 16, D], BF16)
            Wo = self.sbt(st, "Wo", [128, 8, D], BF16)
            mx = self.sbt(st, "mx", [128, 8, 512], BF16)
            h1t = [self.sbt(st, "h1t%d" % i, [128, D], F32) for i in range(2)]
            hsb = [self.sbt(st, "chs%d" % i, [128, D], F32) for i in range(2)]
            cstat = self.sbt(st, "cstat", [128, 2, 4], F32)
            x2 = [self.sbt(st, "x2T%d" % i, [128, 8, 512], BF16) for i in range(2)]
            fT = self.sbt(st, "fT", [128, 16, 512], BF16)
            rl = [self.sbt(st, "rl%d" % i, [128, 512], BF16) for i in range(2)]
            tmpc = [self.sbt(st, "tmpc%d" % i, [128, D], F32) for i in range(2)]
            tmpp = [self.sbt(st, "tmpp%d" % i, [128, 512], F32) for i in range(2)]
            hio = [self.sbt(st, "hio%d" % i, [128, D], F32) for i in range(2)]
            hpi = [self.sbt(st, "hpi%d" % i, [128, D], F32) for i in range(2)]
            wo_src = self.I["w_out"][l]
            for k in range(8):
                em.dma(out=Wo[:, k, :], in_=wo_src[k * 128:(k + 1) * 128, :], q="pool")
            G, shcol = self.G2, self.mcol[:, 2]
            cnt = {"io": 0, "p": 0}

            def prep(b0, bl, xb):
                r = 0 if b0 < NCTX else 1
                nsub = bl // 128
                em.dma(out=mx[:, :, 0:bl], in_=mixT[:, b0:b0 + bl].rearrange("(k p) t -> p k t", p=128))
                yield
                for s_ in range(nsub):
                    i = cnt["p"] % 2
                    cnt["p"] += 1
                    ht, h1, hs_, stt_ = hpi[i], h1t[i], hsb[i], cstat[:, i, :]
                    t0 = b0 + s_ * 128
                    em.dma(out=ht[:], in_=hsrc[t0:t0 + 128, :])
                    yield
                    for nh in range(2):
                        pst = self.ps[2]
                        for k in range(8):
                            em.mm(out=pst[:, :], lhsT=mx[:, k, s_ * 128:(s_ + 1) * 128],
                                  rhs=Wo[:, k, nh * 512:(nh + 1) * 512], start=(k == 0), stop=(k == 7))
                        yield
                        tc_ = tmpp[nh]
                        em.tt("dve", tc_[:], pst[:, :], self.gate[:, 0, r, nh * 512:(nh + 1) * 512], ALU.mult)
                        yield
                        em.tt("pool", h1[:, nh * 512:(nh + 1) * 512], tc_[:], ht[:, nh * 512:(nh + 1) * 512], ALU.add)
                        yield
                    em.dma(out=S["hbuf"][t0:t0 + 128, :], in_=h1[:])
                    em.act(out=hs_[:], in_=h1[:], func=AF.Square, accum_out=stt_[:, 0:1])
                    yield
                    em.ts("dve", stt_[:, 1:2], stt_[:, 0:1], 1.0 / D, EPS, ALU.mult, ALU.add)
                    yield
                    em.act(out=stt_[:, 2:3], in_=stt_[:, 1:2], func=AF.Sqrt)
                    yield
                    em.op("dve", "reciprocal", out=stt_[:, 3:4], in_=stt_[:, 2:3])
                    yield
                    em.act(out=hs_[:], in_=h1[:], func=AF.Identity, scale=stt_[:, 3:4])
                    yield
                    for half in range(2):
                        pst = self.ps[3]
                        for j in range(4):
                            k = half * 4 + j
                            em.tr(out=pst[:, j * 128:(j + 1) * 128], in_=hs_[:, k * 128:(k + 1) * 128], identity=self.ident)
                        yield
                        for j in range(4):
                            k = half * 4 + j
                            o = xb[:, k, s_ * 128:(s_ + 1) * 128]
                            if j % 2 == 0:
                                em.ts("dve", o, pst[:, j * 128:(j + 1) * 128], G[:, k, r:r + 1], shcol[:, k, r:r + 1],
                                      ALU.mult, ALU.add)
                            else:
                                em.act(out=o, in_=pst[:, j * 128:(j + 1) * 128], func=AF.Identity,
                                       scale=G[:, k, r:r + 1], bias=shcol[:, k, r:r + 1])
                        yield
                em.dma(out=S["x2T"][:, b0:b0 + bl].rearrange("(k p) t -> p k t", p=128), in_=xb[:, :, 0:bl])
                yield

            def advance(g, n):
                if g is None:
                    return None
                for _ in range(n):
                    try:
                        next(g)
                    except StopIteration:
                        return None
                return g

            for half in range(2):
                w1 = self.I["w_ff1"][l]
                w2 = self.I["w_ff2"][l]
                for k in range(8):
                    em.dma(out=W1[:, k, :], in_=w1[k * 128:(k + 1) * 128, half * 2048:(half + 1) * 2048], q="pool")
                for f in range(16):
                    r0 = half * 2048 + f * 128
                    em.dma(out=W2[:, f, :], in_=w2[r0:r0 + 128, :], q="pool")
                if half == 0:
                    advance(prep(blocks[0][0], blocks[0][1], x2[0]), 10 ** 6)
                for bi, (b0, bl) in enumerate(blocks):
                    r = 0 if b0 < NCTX else 1
                    nsub = bl // 128
                    xb = x2[bi % 2]
                    nxt = None
                    if half == 0:
                        if bi + 1 < len(blocks):
                            nxt = prep(blocks[bi + 1][0], blocks[bi + 1][1], x2[(bi + 1) % 2])
                    else:
                        if bi == 0:
                            em.dma(out=xb[:, :, 0:bl], in_=S["x2T"][:, b0:b0 + bl].rearrange("(k p) t -> p k t", p=128))
                        if bi + 1 < len(blocks):
                            nb0, nbl = blocks[bi + 1]
                            em.dma(out=x2[(bi + 1) % 2][:, :, 0:nbl],
                                   in_=S["x2T"][:, nb0:nb0 + nbl].rearrange("(k p) t -> p k t", p=128))
                    for f in range(16):
                        pst = self.ps[4 + f % 4]
                        for k in range(8):
                            em.mm(out=pst[:, 0:bl], lhsT=W1[:, k, f * 128:(f + 1) * 128], rhs=xb[:, k, 0:bl],
                                  start=(k == 0), stop=(k == 7))
                        rt = rl[f % 2]
                        if f % 2 == 0:
                            em.act(out=rt[:, 0:bl], in_=pst[:, 0:bl], func=AF.Relu)
                        else:
                            em.ts("dve", rt[:, 0:bl], pst[:, 0:bl], 0.0, None, ALU.max)
                        em.tt("pool", fT[:, f, 0:bl], rt[:, 0:bl], rt[:, 0:bl], ALU.mult)
                        nxt = advance(nxt, 2)
                    for s_ in range(nsub):
                        t0 = b0 + s_ * 128
                        ht = hio[cnt["io"] % 2]
                        cnt["io"] += 1
                        em.dma(out=ht[:], in_=S["hbuf"][t0:t0 + 128, :])
                        ot = tmpc[s_ % 2]
                        for nh in range(2):
                            pst = self.ps[nh]
                            for f in range(16):
                                em.mm(out=pst[:, :], lhsT=fT[:, f, s_ * 128:(s_ + 1) * 128],
                                      rhs=W2[:, f, nh * 512:(nh + 1) * 512], start=(f == 0), stop=(f == 15))
                            em.tt("dve", ot[:, nh * 512:(nh + 1) * 512], pst[:, :],
                                  self.gate[:, 1, r, nh * 512:(nh + 1) * 512], ALU.mult)
                            nxt = advance(nxt, 3)
                        em.tt("pool", ot[:], ot[:], ht[:], ALU.add)
                        if last and half == 1:
                            em.dma(out=self.out[t0 - NCTX:t0 - NCTX + 128, :], in_=ot[:])
                        else:
                            em.dma(out=S["hbuf"][t0:t0 + 128, :], in_=ot[:])
                    advance(nxt, 10 ** 6)
            em.fence()

    def finish(self):
        em = self.em
        em.fence()


def build(nlayers=DEPTH, dbg=False):
    b = Builder(nlayers, dbg)
    b.phase0_once()
    for l in range(nlayers):
        b.phase0(l)
        b.phaseA(l)
        b.mixer_gdn(l)
        b.mixer_s5(l)
        b.mixer_ssd_mla(l)
        b.phaseC(l)
    b.finish()
    return b


def make_in_maps(inputs):
    consts = make_consts()
    pos = np.arange(NLAT)
    row = (pos // 64).astype(np.float32)
    colp = (pos % 64).astype(np.float32)
    inv_freq = (np.float32(10000.0) ** (-np.arange(8, dtype=np.float32) / np.float32(8))).astype(np.float32)
    ang = np.concatenate([row[:, None] * inv_freq, colp[:, None] * inv_freq], axis=-1).astype(np.float32)
    rope = np.concatenate([np.cos(ang), np.sin(ang)], axis=-1).astype(np.float32)
    f = lambda k: np.ascontiguousarray(np.asarray(inputs[k], dtype=np.float32))
    shared = {"consts": consts, "rope": rope}
    for k in ("w_mod", "b_mod", "norm1_g", "norm2_g", "w_in", "w_out", "w_ff1", "w_ff2", "gdn_conv_w", "gdn_norm_g",
              "s5_a_re", "s5_a_im", "s5_log_step", "s5_b_re", "s5_b_im", "s5_c_re", "s5_c_im", "s5_d", "s5_w_glu",
              "s5_b_glu", "ssd_conv_w", "ssd_conv_b", "ssd_d", "ssd_norm_g", "mla_q_norm_g", "mla_kv_norm_g",
              "mla_w_uq", "mla_w_ukv", "mla_q_gain", "mla_k_gain"):
        shared[k] = f(k)
    for k in ("gdn_a_log", "gdn_dt_bias", "ssd_a_log", "ssd_dt_bias"):
        shared[k] = f(k).reshape(DEPTH, 8)
    x, c, ctx, c_ctx = f("x"), f("c"), f("ctx"), f("c_ctx")
    maps = []
    for b in range(x.shape[0]):
        m = dict(shared)
        m["hcat"] = np.ascontiguousarray(np.concatenate([ctx[b], x[b]], axis=0))
        m["cond2"] = np.ascontiguousarray(np.stack([c_ctx, c[b]], axis=0))
        maps.append(m)
    return maps


def kernel(**inputs):
    maps = make_in_maps(inputs)
    b = build()
    res = run_bass_kernel_spmd(b.nc, maps, core_ids=list(range(len(maps))))
    return np.stack([r["out"] for r in res.results], axis=0)
```
